# Optimizing a Trainium2 kernel written in Bass

```python
import jax, jax.numpy as jnp
from jax import lax
import numpy as np

D_MODEL = 4096
BATCH = 32
SEQ = 256
DEPTH = 1
DEC_BATCH = 8
DEC_SEQ = 1024
PAST_LEN = 512

GRID_W = 64
N_HEADS = 16
QK_NOPE = 128
QK_ROPE = 64
V_DIM = 128
QK_DIM = QK_NOPE + QK_ROPE
Q_LORA = 1024
KV_LORA = 512
ROPE_PAIRS_PER_AXIS = QK_ROPE // 4
ROPE_THETA = 10000.0
SOFTMAX_SCALE = QK_DIM ** -0.5
Q_BLOCK = 128
POOL_WIDTH = D_MODEL // 2
POOL_WINDOWS = (2, 4, 8, 16)
N_POOL_GROUPS = len(POOL_WINDOWS)
POOL_GW = POOL_WIDTH // N_POOL_GROUPS
D_FF = 128 * ((8 * D_MODEL // 3 + 127) // 128)
N_MOD = 9
EPS = 1e-6
OFF_Q = POOL_WIDTH
OFF_KV = OFF_Q + Q_LORA
OFF_KPE = OFF_KV + KV_LORA
OFF_GP = OFF_KPE + QK_ROPE
OFF_GA = OFF_GP + D_MODEL
IN_COLS = OFF_GA + D_MODEL

kernel_name = "hybrid_pool_mla_diffusion_step"


def _rmsnorm(x, g):
    xf = x.astype(jnp.float32)
    y = xf * lax.rsqrt(jnp.mean(xf * xf, axis=-1, keepdims=True) + EPS)
    return y.astype(x.dtype) * g


def _swiglu(h, w_in, w_out):
    gu = h @ w_in
    return (jax.nn.silu(gu[..., :D_FF]) * gu[..., D_FF:]) @ w_out


def _axial_rope_tables(rows):
    t = jnp.arange(rows * GRID_W)
    row = (t // GRID_W).astype(jnp.float32)
    col = (t % GRID_W).astype(jnp.float32)
    inv = ROPE_THETA ** (-jnp.arange(ROPE_PAIRS_PER_AXIS, dtype=jnp.float32) / ROPE_PAIRS_PER_AXIS)
    ang = jnp.concatenate([row[:, None] * inv, col[:, None] * inv], axis=-1)
    return jnp.cos(ang), jnp.sin(ang)


def _apply_rope(x, cos, sin):
    half = QK_ROPE // 2
    x1, x2 = x[..., :half], x[..., half:]
    cos = cos.astype(x.dtype)
    sin = sin.astype(x.dtype)
    return jnp.concatenate([x1 * cos - x2 * sin, x1 * sin + x2 * cos], axis=-1)


def _multiscale_pool(u, w_grp, scale):
    B, L, _ = u.shape
    ug = u.reshape(B, L, N_POOL_GROUPS, POOL_GW)
    cs = jnp.concatenate([jnp.zeros((B, 1, N_POOL_GROUPS, POOL_GW), jnp.float32),
                          jnp.cumsum(ug.astype(jnp.float32), axis=1)], axis=1)
    t = jnp.arange(L)
    outs = []
    for g, w in enumerate(POOL_WINDOWS):
        lo = jnp.clip(t - w // 2, 0, L)
        hi = jnp.clip(t + w // 2, 0, L)
        csg = cs[:, :, g]
        win_sum = jnp.take(csg, hi, axis=1) - jnp.take(csg, lo, axis=1)
        cnt = (hi - lo).astype(jnp.float32)[None, :, None]
        outs.append(win_sum / cnt - ug[:, :, g].astype(jnp.float32))
    pooled = jnp.stack(outs, axis=2).astype(u.dtype)
    mixed = jnp.einsum('blgc,gcd->blgd', pooled, w_grp)
    return mixed.reshape(B, L, POOL_WIDTH) * scale


def _mixer_inputs(h, w_in, g_qa, w_qb, g_kva):
    z = h @ w_in
    u = z[..., :OFF_Q]
    q_lat = z[..., OFF_Q:OFF_KV]
    kv_lat = z[..., OFF_KV:OFF_KPE]
    k_pe = z[..., OFF_KPE:OFF_GP]
    gate_pool = z[..., OFF_GP:OFF_GA]
    gate_attn = z[..., OFF_GA:]
    q = (_rmsnorm(q_lat, g_qa) @ w_qb).reshape(*h.shape[:-1], N_HEADS, QK_DIM)
    ckv = _rmsnorm(kv_lat, g_kva)
    return u, q[..., :QK_NOPE], q[..., QK_NOPE:], ckv, k_pe, gate_pool, gate_attn


def _expand_kv(ckv, w_kvb):
    B, L, _ = ckv.shape
    kv = (ckv @ w_kvb).reshape(B, L, N_HEADS, QK_NOPE + V_DIM)
    return kv[..., :QK_NOPE], kv[..., QK_NOPE:]


def _attend(q_nope, q_pe, k_nope, k_pe, v):
    B, S, H, _ = q_nope.shape
    nblk = S // Q_BLOCK
    qn = q_nope.reshape(B, nblk, Q_BLOCK, H, QK_NOPE).swapaxes(0, 1)
    qp = q_pe.reshape(B, nblk, Q_BLOCK, H, QK_ROPE).swapaxes(0, 1)

    def block(args):
        bn, bp = args
        s = jnp.einsum('bqhd,bkhd->bhqk', bn, k_nope) + jnp.einsum('bqhd,bkd->bhqk', bp, k_pe)
        p = jax.nn.softmax(s.astype(jnp.float32) * SOFTMAX_SCALE, axis=-1).astype(v.dtype)
        return jnp.einsum('bhqk,bkhd->bqhd', p, v)

    o = lax.map(block, (qn, qp))
    return o.swapaxes(0, 1).reshape(B, S, H, V_DIM)


def _layer(x, mod, lw, rope=None, ctx_ckv=None, ctx_kpe=None):
    (g_pre, g_post, w_f1_in, w_f1_out, w_f2_in, w_f2_out, w_in, g_qa, w_qb, g_kva, w_kvb,
     w_pool_grp, pool_scale, w_pool_out, w_attn_out, w_o) = lw

    def pre(y, k):
        return _rmsnorm(y, g_pre[k]) * (1 + mod[:, :, 3 * k + 1]) + mod[:, :, 3 * k]

    def post(y, k):
        return mod[:, :, 3 * k + 2] * _rmsnorm(y, g_post[k])

    x = x + 0.5 * post(_swiglu(pre(x, 0), w_f1_in, w_f1_out), 0)
    h = pre(x, 1)
    u, q_nope, q_pe, ckv, k_pe, gate_pool, gate_attn = _mixer_inputs(h, w_in, g_qa, w_qb, g_kva)
    if ctx_ckv is None:
        keys_ckv, keys_pe = ckv, k_pe
    else:
        cos, sin = rope
        q_pe = _apply_rope(q_pe, cos[:, None], sin[:, None])
        keys_ckv = jnp.concatenate([ctx_ckv, ckv], axis=1)
        keys_pe = jnp.concatenate([ctx_kpe, _apply_rope(k_pe, cos, sin)], axis=1)
    k_nope, v = _expand_kv(keys_ckv, w_kvb)
    o = _attend(q_nope, q_pe, k_nope, keys_pe, v)
    y_pool = _multiscale_pool(u, w_pool_grp, pool_scale) @ w_pool_out
    y_attn = o.reshape(o.shape[0], o.shape[1], N_HEADS * V_DIM) @ w_attn_out
    y = (jax.nn.sigmoid(gate_pool) * y_pool + jax.nn.sigmoid(gate_attn) * y_attn) @ w_o
    x = x + post(y, 1)
    x = x + 0.5 * post(_swiglu(pre(x, 2), w_f2_in, w_f2_out), 2)
    return x, ckv, k_pe


def setup_inputs(seed: int = 0) -> dict:
    key = jax.random.key(seed)
    ks = jax.random.split(key, 32)
    f32 = jnp.float32

    def nrm(k, shape, scale):
        return jax.random.normal(k, shape, f32) * scale

    D = D_MODEL
    return {
        "x_prompt": nrm(ks[0], (BATCH, SEQ, D), 1.0),
        "x_sample": nrm(ks[1], (DEC_BATCH, DEC_SEQ, D), 1.0),
        "cache_ckv": nrm(ks[2], (DEC_BATCH, DEPTH, PAST_LEN, KV_LORA), 1.0),
        "cache_kpe": nrm(ks[3], (DEC_BATCH, DEPTH, PAST_LEN, QK_ROPE), 1.0),
        "c": nrm(ks[4], (DEC_BATCH, D), 1.0),
        "c_ctx": nrm(ks[5], (D,), 1.0),
        "w_mod": nrm(ks[6], (DEPTH, D, N_MOD * D), 0.5 * D ** -0.5),
        "b_mod": nrm(ks[7], (DEPTH, N_MOD * D), 0.01),
        "g_pre": 1.0 + nrm(ks[8], (DEPTH, 3, D), 0.05),
        "g_post": 1.0 + nrm(ks[9], (DEPTH, 3, D), 0.05),
        "w_ffn1_in": nrm(ks[10], (DEPTH, D, 2 * D_FF), D ** -0.5),
        "w_ffn1_out": nrm(ks[11], (DEPTH, D_FF, D), D_FF ** -0.5),
        "w_ffn2_in": nrm(ks[12], (DEPTH, D, 2 * D_FF), D ** -0.5),
        "w_ffn2_out": nrm(ks[13], (DEPTH, D_FF, D), D_FF ** -0.5),
        "w_in": nrm(ks[14], (DEPTH, D, IN_COLS), D ** -0.5),
        "g_qa": 1.0 + nrm(ks[15], (DEPTH, Q_LORA), 0.05),
        "w_qb": nrm(ks[16], (DEPTH, Q_LORA, N_HEADS * QK_DIM), Q_LORA ** -0.5),
        "g_kva": 1.0 + nrm(ks[17], (DEPTH, KV_LORA), 0.05),
        "w_kvb": nrm(ks[18], (DEPTH, KV_LORA, N_HEADS * (QK_NOPE + V_DIM)), KV_LORA ** -0.5),
        "w_pool_grp": nrm(ks[19], (DEPTH, N_POOL_GROUPS, POOL_GW, POOL_GW), POOL_GW ** -0.5),
        "pool_scale": 1.0 + nrm(ks[20], (DEPTH, POOL_WIDTH), 0.1),
        "w_pool_out": nrm(ks[21], (DEPTH, POOL_WIDTH, D), POOL_WIDTH ** -0.5),
        "w_attn_out": nrm(ks[22], (DEPTH, N_HEADS * V_DIM, D), (N_HEADS * V_DIM) ** -0.5),
        "w_o": nrm(ks[23], (DEPTH, D, D), D ** -0.5),
    }


def reference(x_prompt, x_sample, cache_ckv, cache_kpe, c, c_ctx, w_mod, b_mod, g_pre, g_post,
              w_ffn1_in, w_ffn1_out, w_ffn2_in, w_ffn2_out, w_in, g_qa, w_qb, g_kva, w_kvb,
              w_pool_grp, pool_scale, w_pool_out, w_attn_out, w_o):
    rows = x_sample.shape[1] // GRID_W
    rope = _axial_rope_tables(rows)
    xp, xs = x_prompt, x_sample
    new_ckv, new_kpe = [], []
    for l in range(DEPTH):
        lw = (g_pre[l], g_post[l], w_ffn1_in[l], w_ffn1_out[l], w_ffn2_in[l], w_ffn2_out[l],
              w_in[l], g_qa[l], w_qb[l], g_kva[l], w_kvb[l], w_pool_grp[l], pool_scale[l],
              w_pool_out[l], w_attn_out[l], w_o[l])
        mod_ctx = (jax.nn.silu(c_ctx[None]) @ w_mod[l] + b_mod[l]).reshape(1, 1, N_MOD, D_MODEL)
        mod_lat = (jax.nn.silu(c) @ w_mod[l] + b_mod[l]).reshape(c.shape[0], 1, N_MOD, D_MODEL)
        xp, ckv_l, kpe_l = _layer(xp, mod_ctx, lw)
        new_ckv.append(ckv_l)
        new_kpe.append(kpe_l)
        xs, _, _ = _layer(xs, mod_lat, lw, rope, cache_ckv[:, l], cache_kpe[:, l])
    state_ckv = jnp.stack(new_ckv, axis=1)
    state_kpe = jnp.stack(new_kpe, axis=1)
    return (xp, xs, state_ckv, state_kpe)
```

```python
import math
from contextlib import ExitStack

import numpy as np

import concourse.bass as bass
import concourse.mybir as mybir
from concourse.bass_utils import run_bass_kernel_spmd

F32 = mybir.dt.float32
BF16 = mybir.dt.bfloat16
ALU = mybir.AluOpType
AF = mybir.ActivationFunctionType

D = 4096
DFF = 11008
NT = 1024
NMOD = 9
EPS = 1e-6
QL, KVL, ROPE = 1024, 512, 64
NH, DN, DV = 16, 128, 128
QKD = DN + ROPE
OFF_Q = 2048
OFF_KV = OFF_Q + QL
OFF_KPE = OFF_KV + KVL
OFF_GP = OFF_KPE + ROPE
OFF_GA = OFF_GP + D
INC = OFF_GA + D
SCALE = QKD ** -0.5
PAST = 512
WINS = (2, 4, 8, 16)


class Buf:
    __slots__ = ("name", "w", "r")

    def __init__(self, name):
        self.name = name
        self.w = None
        self.r = {}


class Op:
    __slots__ = ("eng", "fn", "reads", "writes", "ndma", "chan", "deps", "sig", "waits")


class Prog:
    ENGS = ("pe", "act", "dve", "pool", "sp")
    BLK = {"pe": "tensor", "act": "scalar", "dve": "vector", "pool": "gpsimd", "sp": "sync"}
    ROT = 20000

    def __init__(self):
        self.ops = []

    def op(self, eng, fn, reads=(), writes=(), ndma=0, chan=None):
        o = Op()
        o.eng, o.fn, o.reads, o.writes, o.ndma, o.chan = eng, fn, tuple(reads), tuple(writes), ndma, chan
        o.sig = None
        o.waits = ()
        if ndma:
            assert chan is not None
        self.ops.append(o)
        return o

    def plan(self):
        ops = self.ops
        needs = set()
        for i, o in enumerate(ops):
            deps = set()
            for b in o.reads:
                if b.w is not None:
                    deps.add(b.w)
            for b in o.writes:
                if b.w is not None:
                    deps.add(b.w)
                deps.update(b.r.values())
            deps.discard(i)
            if o.eng == "pe" and not o.ndma:
                deps = {d for d in deps if not (ops[d].eng == "pe" and not ops[d].ndma)}
            o.deps = deps
            needs.update(deps)
            key = ("dma", o.chan) if o.ndma else o.eng
            for b in o.reads:
                b.r[key] = i
            for b in o.writes:
                b.w = i
                b.r = {}
        eng_cnt = {e: 0 for e in self.ENGS}
        eng_gen = {e: 0 for e in self.ENGS}
        chan_cnt = {}
        self.sem_keys = []
        seen = set()

        def reg(k):
            if k not in seen:
                seen.add(k)
                self.sem_keys.append(k)

        for i, o in enumerate(ops):
            if o.ndma:
                c = chan_cnt.get(o.chan, 0) + 16 * o.ndma
                chan_cnt[o.chan] = c
                o.sig = (("dma", o.chan), c)
                reg(o.sig[0])
            elif i in needs:
                if eng_cnt[o.eng] >= self.ROT:
                    eng_cnt[o.eng] = 0
                    eng_gen[o.eng] += 1
                eng_cnt[o.eng] += 1
                o.sig = ((o.eng, eng_gen[o.eng]), eng_cnt[o.eng])
                reg(o.sig[0])
        self.final_dma = dict(chan_cnt)
        waited = {e: {} for e in self.ENGS}
        for o in ops:
            w = {}
            for d in o.deps:
                s, v = ops[d].sig
                if w.get(s, 0) < v:
                    w[s] = v
            wl = []
            we = waited[o.eng]
            for s, v in w.items():
                if we.get(s, 0) < v:
                    we[s] = v
                    wl.append((s, v))
            o.waits = wl

    def emit(self, nc, st):
        self.plan()
        sems = {}
        for n, k in enumerate(self.sem_keys):
            sems[k] = st.enter_context(nc.semaphore(f"s{n}"))
        per = {e: [o for o in self.ops if o.eng == e] for e in self.ENGS}
        with nc.Block() as block:
            for eng in self.ENGS:
                lst = per[eng]

                def body(e, lst=lst, eng=eng):
                    for o in lst:
                        for (s, v) in o.waits:
                            e.wait_ge(sems[s], v)
                        if o.ndma:
                            o.fn(e, sems[o.sig[0]])
                        else:
                            inst = o.fn(e)
                            if o.sig is not None:
                                inst.then_inc(sems[o.sig[0]], 1)
                    if eng == "sp":
                        for ch, c in self.final_dma.items():
                            e.wait_ge(sems[("dma", ch)], c)

                getattr(block, self.BLK[eng])(body)


class Region:
    def __init__(self, nc, st, name, nbytes):
        self.t = st.enter_context(nc.sbuf_tensor(name, [128, nbytes // 2], BF16))
        self.name = name
        self.bufs = [Buf(f"{name}.{i}") for i in range(nbytes // 1024)]

    def v(self, dtype, off, n, parts=128):
        es = 4 if dtype is F32 else 2
        assert off % 4 == 0 and off + n * es <= len(self.bufs) * 1024, (self.name, off, n)
        a = self.t[0:parts, off // 2: off // 2 + n * es // 2]
        if dtype is F32:
            a = a.bitcast(F32)
        return a, self.bufs[off // 1024: (off + n * es + 1023) // 1024]


def build_nc(stop_after=None, dbg=False):
    nc = bass.Bass("TRN2", target_bir_lowering=False)
    P = Prog()

    def din(name, shape, dt=F32):
        return nc.dram_tensor(name, list(shape), dt, kind="ExternalInput").ap()

    def dscr(name, shape, dt=F32, out=False):
        return nc.dram_tensor(name, list(shape), dt, kind="ExternalOutput" if out else "Internal").ap()

    x_d = din("x", [2 * NT, D])
    cckv_d = din("cckv", [PAST, KVL])
    ckpe_d = din("ckpe", [PAST, ROPE])
    cvec_d = din("cvec", [2, D])
    wmod_d = din("w_mod", [D, NMOD * D])
    bmod_d = din("b_mod", [288, 128])
    gpre_d = din("g_pre", [96, 128])
    gpost_d = din("g_post", [96, 128])
    wfi_d = [din("w_f1i", [D, 2 * DFF]), din("w_f2i", [D, 2 * DFF])]
    wfo_d = [din("w_f1o", [DFF, D]), din("w_f2o", [DFF, D])]
    win_d = din("w_in", [D, INC])
    gqa_d = din("g_qa", [1, QL])
    wqb_d = din("w_qb", [QL, NH * QKD])
    gkva_d = din("g_kva", [1, KVL])
    wkvb_d = din("w_kvb", [KVL, NH * (DN + DV)])
    wpg_d = din("w_pg", [2048, 512])
    psc_d = din("pool_scale", [16, 128])
    wpo_d = din("w_po", [2048, D])
    wao_d = din("w_ao", [2048, D])
    wo_d = din("w_o", [D, D])
    ident_d = din("ident", [128, 128])
    ropeT_d = din("ropeT", [2, 64, NT])
    ropeK_d = din("ropeK", [NT, 96])
    invc_d = din("invc", [2, 4 * NT])

    y_d = dscr("y", [2 * NT, D], out=True)
    sckv_d = dscr("sckv", [NT, KVL], out=True)
    skpe_d = dscr("skpe", [NT, ROPE], out=True)
    x1_d = dscr("x1s", [2 * NT, D], out=dbg)
    x2_d = dscr("x2s", [2 * NT, D], out=dbg)
    actT_d = dscr("actT", [86, 128, NT], BF16)
    sg_d = dscr("sgs", [64, 128, NT])
    mix_d = dscr("mixs", [16, 128, NT], BF16, out=False)

    B_x1 = [[Buf(f"x1.{g}.{t}") for t in range(8)] for g in range(2)]
    B_x2 = [[Buf(f"x2.{g}.{t}") for t in range(8)] for g in range(2)]
    B_actT = [Buf(f"actT.{i}") for i in range(43)]
    B_sg = [Buf(f"sg.{i}") for i in range(64)]
    B_mix = [Buf(f"mix.{i}") for i in range(16)]

    with ExitStack() as st:
        RH = Region(nc, st, "RH", 64 * 1024)
        RW = Region(nc, st, "RW", 64 * 1024)
        RA = Region(nc, st, "RA", 64 * 1024)
        ps = st.enter_context(nc.psum_tensor("ps", [128, 4096], F32))
        PB = [Buf(f"pb{i}") for i in range(8)]

        def bank(i):
            return ps[:, i * 512:(i + 1) * 512]

        bank_ctr = [0]

        def next_bank():
            b = bank_ctr[0] % 7
            bank_ctr[0] += 1
            return b

        def ctile(name, shape, dt=F32):
            return st.enter_context(nc.sbuf_tensor("c_" + name, list(shape), dt)), Buf(name)

        ident, B_ident = ctile("ident", [128, 128])
        onesf, B_onesf = ctile("onesf", [128, 128])
        onesb, B_onesb = ctile("onesb", [128, 128], BF16)
        modT, B_modT = ctile("modT", [128, 288, 2])
        vecT, B_vecT = ctile("vecT", [128, 512])
        ABG, B_ABG = ctile("ABG", [128, 2 * 3 * 2 * 32])
        sT, B_sT = ctile("sT", [128, 64], BF16)
        stat, _ = ctile("stat", [128, 64])
        B_stat = [Buf(f"stat{i}") for i in range(64)]
        stat_ctr = [0]

        def next_stat():
            i = stat_ctr[0] % 64
            stat_ctr[0] += 1
            return i

        bmT = vecT[:, 0:288]
        gpreT = vecT[:, 288:384]
        gpostT = vecT[:, 384:480]
        pscT = vecT[:, 480:496]

        def A_ap(g, k):
            o = ((g * 3 + k) * 2) * 32
            return ABG[:, o:o + 32]

        def G_ap(g, k):
            o = ((g * 3 + k) * 2 + 1) * 32
            return ABG[:, o:o + 32]

        def Bv_ap(g, k, j):
            return modT[:, (3 * k) * 32 + j, g:g + 1]

        def dma(eng, out_ap, in_ap, reads, writes, chan):
            def fn(e, sem):
                e.dma_start(out=out_ap, in_=in_ap).then_inc(sem, 16)
            P.op(eng, fn, reads=reads, writes=writes, ndma=1, chan=chan)

        def mm_group(out_ap, bk, pairs, reads):
            def fn(e):
                n = len(pairs)
                inst = None
                for i, (l, r) in enumerate(pairs):
                    inst = e.matmul(out_ap, l, r, start=(i == 0), stop=(i == n - 1))
                return inst
            P.op("pe", fn, reads=reads, writes=[PB[bk]])

        def tr_group(bk, items, reads):
            def fn(e):
                inst = None
                for o, i, idn in items:
                    inst = e.transpose(o, i, idn)
                return inst
            P.op("pe", fn, reads=list(reads) + [B_ident], writes=[PB[bk]])

        def rstd_ops(ss_i, n):
            a = stat[:, ss_i:ss_i + 1]

            P.op("act", lambda e: e.activation(a, a, AF.Sqrt, bias=float(EPS), scale=1.0 / n),
                 reads=[], writes=[B_stat[ss_i]])
            P.op("dve", lambda e: e.reciprocal(a, a), reads=[], writes=[B_stat[ss_i]])

        mrow, B_mrow = ctile("mrow", [2, 512])

        def mod_rows(w3, wb, k0, nk, first, last):
            sT3 = sT[:, :].rearrange("p (k g) -> p k g", g=2)

            def fn(e):
                inst = None
                for k in range(nk):
                    inst = e.matmul(bank(7)[0:2, :], sT3[:, k0 + k, :], w3[:, k, :],
                                    start=(first and k == 0), stop=(last and k == nk - 1), skip_group_check=True)
                return inst
            P.op("pe", fn, reads=list(wb) + [B_sT], writes=[PB[7]])

        def mod_finish(cb):
            P.op("act", lambda e: e.activation(mrow[:, :], bank(7)[0:2, :], AF.Copy), writes=[PB[7], B_mrow])
            bk = next_bank()
            tr_group(bk, [(bank(bk)[:, 2 * ct:2 * ct + 2], mrow[0:2, ct * 128:(ct + 1) * 128], ident[0:2, 0:2])
                          for ct in range(4)], [B_mrow])

            def ev(e):
                return e.tensor_tensor(
                    modT[:, cb * 4:(cb + 1) * 4, :],
                    bank(bk)[:, 0:8].rearrange("p (c g) -> p c g", g=2),
                    bmT[:, cb * 4:(cb + 1) * 4].unsqueeze(2).to_broadcast([128, 4, 2]),
                    ALU.add)
            P.op("dve", ev, reads=[B_vecT], writes=[PB[bk], B_modT])

        def stage0():
            dma("sp", ident[:, :], ident_d[:, :], [], [B_ident], "ident")
            P.op("dve", lambda e: e.memset(onesf[:, :], 1.0), writes=[B_onesf])
            P.op("dve", lambda e: e.memset(onesb[:, :], 1.0), writes=[B_onesb])
            c2, c2b = RA.v(F32, 0, D, parts=2)
            dma("sp", c2, cvec_d[:, :], [], c2b, "c2")
            P.op("act", lambda e: e.activation(c2, c2, AF.Silu), writes=c2b)
            bk = next_bank()
            tr_group(bk, [(bank(bk)[:, 2 * i:2 * i + 2], c2[:, i * 128:(i + 1) * 128], ident[0:2, 0:2])
                          for i in range(32)], c2b)
            P.op("dve", lambda e, bk=bk: e.tensor_copy(sT[:, :], bank(bk)[:, 0:64]), writes=[PB[bk], B_sT])
            tmp, tmpb = RA.v(F32, 16384, 6 * 128)
            srcs = [(bmod_d[0:96, :], 96), (bmod_d[96:192, :], 96), (bmod_d[192:288, :], 96),
                    (gpre_d[:, :], 96), (gpost_d[:, :], 96), (psc_d[:, :], 16)]
            for i, (s, n) in enumerate(srcs):
                dma("sp", tmp[0:n, i * 128:(i + 1) * 128], s, [], tmpb, "tmpv")
            bk2 = next_bank()
            items = []
            col = 0
            for i, (s, n) in enumerate(srcs):
                items.append((bank(bk2)[:, col:col + n], tmp[0:n, i * 128:(i + 1) * 128], ident[0:n, 0:n]))
                col += n
            tr_group(bk2, items, tmpb)
            P.op("dve", lambda e: e.tensor_copy(vecT[:, 0:496], bank(bk2)[:, 0:496]), writes=[PB[bk2], B_vecT])
            sT3 = sT[:, :].rearrange("p (k g) -> p k g", g=2)
            for cb in range(16):
                slot = cb % 2
                w, wb = RW.v(BF16, slot * 32768, 32 * 512)
                w3 = w.rearrange("p (k c) -> p k c", c=512)
                dma("pool", w3, wmod_d[:, cb * 512:(cb + 1) * 512].rearrange("(k p) c -> p k c", p=128),
                    [], wb, f"RW{slot}")
                mod_rows(w3, wb, 0, 32, True, True)
                mod_finish(cb)
            for g in range(2):
                abg_A(g, 0)

        def abg_A(g, k):
            def fa(e):
                return e.scalar_tensor_tensor(A_ap(g, k), modT[:, (3 * k + 1) * 32:(3 * k + 2) * 32, g], 1.0,
                                              gpreT[:, k * 32:(k + 1) * 32], ALU.add, ALU.mult)
            P.op("dve", fa, reads=[B_modT, B_vecT], writes=[B_ABG])

        def abg_G(g, k):
            ck = 1.0 if k == 1 else 0.5

            def fg(e):
                return e.scalar_tensor_tensor(G_ap(g, k), modT[:, (3 * k + 2) * 32:(3 * k + 3) * 32, g], ck,
                                              gpostT[:, k * 32:(k + 1) * 32], ALU.mult, ALU.mult)
            P.op("dve", fg, reads=[B_modT, B_vecT], writes=[B_ABG])

        mod_pending = [(cb, kh) for cb in range(16, 72) for kh in range(2)]
        mod_state = {"n": 0, "done": False}

        def mod_more(n):
            sT3 = sT[:, :].rearrange("p (k g) -> p k g", g=2)
            for _ in range(n):
                if not mod_pending:
                    break
                cb, kh = mod_pending.pop(0)
                slot = mod_state["n"] % 2
                mod_state["n"] += 1
                w, wb = RA.v(BF16, slot * 16384, 16 * 512)
                w3 = w.rearrange("p (k c) -> p k c", c=512)
                dma("pool", w3, wmod_d[kh * 2048:(kh + 1) * 2048, cb * 512:(cb + 1) * 512]
                    .rearrange("(k p) c -> p k c", p=128), [], wb, f"RAm{slot}")

                mod_rows(w3, wb, kh * 16, 16, kh == 0, kh == 1)
                if kh == 1:
                    mod_finish(cb)
            if not mod_pending and not mod_state["done"]:
                mod_state["done"] = True
                for g in range(2):
                    for k in range(3):
                        if k > 0:
                            abg_A(g, k)
                        abg_G(g, k)

        def build_Gb(g, k, reg, off, scratch_reg, scratch_off):
            Gb, Gbb = reg.v(F32, off, D)
            if scratch_reg is None:
                dg, dgb = dgt[:, :], [B_dgt]
            else:
                dg, dgb = scratch_reg.v(F32, scratch_off, 512)
            for jb in range(8):
                def fd(e, jb=jb):
                    inst = None
                    for i in range(4):
                        j = jb * 4 + i
                        inst = e.tensor_scalar(dg[:, i * 128:(i + 1) * 128], ident[:, :], G_ap(g, k)[:, j:j + 1],
                                               None, ALU.mult)
                    return inst
                P.op("dve", fd, reads=[B_ident, B_ABG], writes=dgb)
                bk = next_bank()

                def fm(e, bk=bk):
                    inst = None
                    for i in range(4):
                        inst = e.matmul(bank(bk)[:, i * 128:(i + 1) * 128], onesf[:, :], dg[:, i * 128:(i + 1) * 128],
                                        start=True, stop=True)
                    return inst
                P.op("pe", fm, reads=list(dgb) + [B_onesf], writes=[PB[bk]])
                P.op("act", lambda e, bk=bk, jb=jb: e.activation(Gb[:, jb * 512:(jb + 1) * 512], bank(bk), AF.Copy),
                     writes=[PB[bk]] + Gbb[jb * 2:jb * 2 + 2])
            return Gb, Gbb

        def pre(g, k, src_d, src_bufs):
            hT, hTb = RH.v(BF16, 0, 32 * NT)
            hT3 = hT.rearrange("p (j t) -> p j t", t=NT)
            junk, junkb = RA.v(BF16, 32768, D)
            for tt in range(8):
                xt, xtb = RA.v(F32, (tt % 2) * 16384, D)
                r0 = g * NT + tt * 128
                dma("sp", xt, src_d[r0:r0 + 128, :], [src_bufs[g][tt]] if src_bufs else [], xtb, f"RAx{tt % 2}")
                si = next_stat()
                P.op("act", lambda e, xt=xt, si=si: e.activation(junk, xt, AF.Square, accum_out=stat[:, si:si + 1]),
                     reads=xtb, writes=junkb + [B_stat[si]])
                rstd_ops(si, D)
                P.op("dve", lambda e, xt=xt, si=si: e.tensor_scalar(xt, xt, stat[:, si:si + 1], None, ALU.mult),
                     reads=[B_stat[si]], writes=xtb)
                for jb in range(8):
                    bk = next_bank()
                    tr_group(bk, [(bank(bk)[:, i * 128:(i + 1) * 128], xt[:, (jb * 4 + i) * 128:(jb * 4 + i + 1) * 128],
                                   ident[:, :]) for i in range(4)], xtb)
                    eng = "act" if jb % 2 == 0 else "dve"

                    def ev(e, jb=jb, bk=bk, tt=tt, eng=eng):
                        inst = None
                        for i in range(4):
                            j = jb * 4 + i
                            o = hT3[:, j, tt * 128:(tt + 1) * 128]
                            s = bank(bk)[:, i * 128:(i + 1) * 128]
                            if eng == "act":
                                inst = e.activation(o, s, AF.Identity, bias=Bv_ap(g, k, j), scale=A_ap(g, k)[:, j:j + 1])
                            else:
                                inst = e.tensor_scalar(o, s, A_ap(g, k)[:, j:j + 1], Bv_ap(g, k, j), ALU.mult, ALU.add)
                        return inst
                    P.op(eng, ev, reads=[B_ABG, B_modT], writes=[PB[bk]] + hTb)
            return hT3, hTb

        def ffn_a(f, hT3, hTb):
            wd = wfi_d[f]
            nblk = 22
            ngrp = [0]

            def load(cb, half):
                ncol = 512 if cb < 21 else 256
                w, wb = RW.v(BF16, half * 32768, 32 * ncol)
                w3 = w.rearrange("p (k c) -> p k c", c=ncol)
                c0 = half * DFF + cb * 512
                dma("pool", w3, wd[:, c0:c0 + ncol].rearrange("(k p) c -> p k c", p=128), [], wb, f"RW{half}")
                return w3, wb

            cur = [load(0, 0), load(0, 1)]
            for cb in range(nblk):
                nct = 4 if cb < 21 else 2
                sg, sgb = RA.v(F32, 32768, 4 * NT)
                sg3 = sg.rearrange("p (c t) -> p c t", t=NT)
                stg, stgb = RA.v(BF16, 49152 + (cb % 2) * 8192, 4 * NT)
                stg3 = stg.rearrange("p (c t) -> p c t", t=NT)
                nxt = [None, None]
                for half in range(2):
                    w3, wb = cur[half]
                    for ct in range(nct):
                        for tb in range(2):
                            if ngrp[0] % 3 == 0:
                                mod_more(1)
                            ngrp[0] += 1
                            bk = next_bank()
                            mm_group(bank(bk), bk, [(w3[:, k, ct * 128:(ct + 1) * 128], hT3[:, k, tb * 512:(tb + 1) * 512])
                                                    for k in range(32)], list(wb) + hTb)
                            sgv = sg3[:, ct, tb * 512:(tb + 1) * 512]
                            sgvb = sgb[ct * 4 + tb * 2: ct * 4 + tb * 2 + 2]
                            if half == 0:
                                P.op("act", lambda e, sgv=sgv, bk=bk: e.activation(sgv, bank(bk), AF.Silu),
                                     writes=[PB[bk]] + sgvb)
                            else:
                                P.op("dve", lambda e, sgv=sgv, bk=bk, ct=ct, tb=tb, stg3=stg3:
                                     e.tensor_tensor(stg3[:, ct, tb * 512:(tb + 1) * 512], bank(bk), sgv, ALU.mult),
                                     reads=sgvb, writes=[PB[bk]] + stgb)
                    if cb + 1 < nblk:
                        nxt[half] = load(cb + 1, half)
                cur = nxt
                dma("sp", actT_d[cb * 4:cb * 4 + nct].rearrange("c p t -> p c t"), stg3[:, 0:nct, :], stgb,
                    B_actT[cb * 2:cb * 2 + 2], f"RAstg{cb % 2}")
            mod_more(1000)

        def post_core(acc_t, accb_t, Gb, Gbb, xt, xtb, chan, junk, junkb, dst_d, dst_b, row0):
            si = next_stat()
            P.op("act", lambda e: e.activation(junk, acc_t, AF.Square, accum_out=stat[:, si:si + 1]),
                 reads=accb_t, writes=junkb + [B_stat[si]])
            rstd_ops(si, D)
            P.op("dve", lambda e: e.scalar_tensor_tensor(acc_t, acc_t, stat[:, si:si + 1], Gb, ALU.mult, ALU.mult),
                 reads=[B_stat[si]] + Gbb, writes=accb_t)
            P.op("dve", lambda e: e.tensor_tensor(xt, xt, acc_t, ALU.add), reads=accb_t, writes=xtb)
            dma("sp", dst_d[row0:row0 + 128, :], xt, xtb, dst_b, chan)

        def post_tile(g, k, acc_t, accb_t, Gb, Gbb, xt, xtb, chan, src_d, src_b, dst_d, dst_b, row0):
            si = next_stat()
            P.op("act", lambda e: e.activation(xt, acc_t, AF.Square, accum_out=stat[:, si:si + 1]),
                 reads=accb_t, writes=xtb + [B_stat[si]])
            rstd_ops(si, D)
            dma("sp", xt, src_d[row0:row0 + 128, :], src_b, xtb, chan)
            P.op("dve", lambda e: e.scalar_tensor_tensor(acc_t, acc_t, stat[:, si:si + 1], Gb, ALU.mult, ALU.mult),
                 reads=[B_stat[si]] + Gbb, writes=accb_t)
            P.op("dve", lambda e: e.tensor_tensor(xt, xt, acc_t, ALU.add), reads=accb_t, writes=xtb)
            dma("sp", dst_d[row0:row0 + 128, :], xt, xtb, dst_b, chan)

        def ffn_b(f, g, k, src_d, src_bufs, dst_d, dst_bufs):
            wd = wfo_d[f]
            Gb, Gbb = build_Gb(g, k, RH, 32768, None, None)
            nsup = 11
            ctr = 0

            def xload(gt, sl):
                xt, xtb = RH.v(F32, sl * 16384, D)
                r0 = g * NT + gt * 128
                dma("sp", xt, src_d[r0:r0 + 128, :], [src_bufs[g][gt]] if src_bufs else [], xtb, f"RHx{sl}")

            for hf in range(2):
                acc, accb = RA.v(F32, 0, 4 * D)
                acc3 = acc.rearrange("p (t n) -> p t n", n=D)
                for nh in range(2):
                    for s in range(nsup):
                        ks = 8 if s < 10 else 6
                        slot = ctr % 2
                        ctr += 1
                        w, wb = RW.v(BF16, slot * 32768, ks * 2048)
                        w3 = w.rearrange("p (k n) -> p k n", n=2048)
                        dma("pool", w3, wd[s * 1024:s * 1024 + ks * 128, nh * 2048:(nh + 1) * 2048]
                            .rearrange("(k p) n -> p k n", p=128), [], wb, f"RW{slot}")
                        if nh == 1 and s == nsup - 1:
                            for tt in range(2):
                                xload(hf * 4 + tt, tt)
                        a, ab = RH.v(BF16, 49152 + slot * 8192, ks * 512)
                        a3 = a.rearrange("p (k t) -> p k t", t=512)
                        dma("sp", a3, actT_d[s * 8:s * 8 + ks, :, hf * 512:(hf + 1) * 512].rearrange("k p t -> p k t"),
                            B_actT[s * 4:s * 4 + 4], ab, f"RHa{slot}")
                        for tt in range(4):
                            bks = [next_bank() for _ in range(4)]

                            def fm(e, bks=bks, tt=tt, ks=ks, a3=a3, w3=w3):
                                inst = None
                                for kk in range(ks):
                                    for i in range(4):
                                        inst = e.matmul(bank(bks[i]), a3[:, kk, tt * 128:(tt + 1) * 128],
                                                        w3[:, kk, i * 512:(i + 1) * 512],
                                                        start=(kk == 0), stop=(kk == ks - 1))
                                return inst
                            P.op("pe", fm, reads=list(wb) + list(ab), writes=[PB[b] for b in bks])
                            for i in range(4):
                                nb = nh * 4 + i
                                o = acc3[:, tt, nb * 512:(nb + 1) * 512]
                                ob = accb[tt * 16 + nb * 2: tt * 16 + nb * 2 + 2]
                                if s == 0:
                                    P.op("act", lambda e, o=o, b=bks[i]: e.activation(o, bank(b), AF.Copy),
                                         writes=[PB[bks[i]]] + ob)
                                else:
                                    P.op("dve", lambda e, o=o, b=bks[i]: e.tensor_tensor(o, o, bank(b), ALU.add),
                                         writes=[PB[bks[i]]] + ob)
                for tt in range(4):
                    gt = hf * 4 + tt
                    xt, xtb = RH.v(F32, (tt % 2) * 16384, D)
                    junk, junkb = RH.v(BF16, 49152 + (tt % 2) * 8192, D)
                    post_core(acc3[:, tt, :], accb[tt * 16:(tt + 1) * 16], Gb, Gbb, xt, xtb, f"RHx{tt % 2}",
                              junk, junkb, dst_d, [dst_bufs[g][gt]] if dst_bufs else [], g * NT + gt * 128)
                    if tt + 2 < 4:
                        xload(gt + 2, tt % 2)

        dgt, B_dgt = ctile("dgt", [128, 512])

        def copy_op(eng, out_ap, in_ap, reads, writes):
            if eng == "act":
                P.op("act", lambda e: e.activation(out_ap, in_ap, AF.Copy), reads=reads, writes=writes)
            else:
                P.op("dve", lambda e: e.tensor_copy(out_ap, in_ap), reads=reads, writes=writes)

        def mixer1(g, hT3, hTb):
            nkeys = NT if g == 0 else NT + PAST
            koff = 0 if g == 0 else PAST
            qnT, qnTb = RW.v(BF16, 32768, 8 * NT)
            qnT3 = qnT.rearrange("p (k t) -> p k t", t=NT)
            ckvT, ckvTb = RW.v(BF16, 49152, 4 * 1536)
            ckvT3 = ckvT.rearrange("p (k t) -> p k t", t=1536)
            kpeT, kpeTb = RW.v(BF16, 61440, 1536, parts=64)
            gqa, gqab = RA.v(F32, 26624, QL)
            gkv, gkvb = RA.v(F32, 30720, KVL)
            dma("sp", gqa, gqa_d.partition_broadcast(128)[:, 0, :], [], gqab, "gqa")
            dma("sp", gkv, gkva_d.partition_broadcast(128)[:, 0, :], [], gkvb, "gkv")
            rk, rkb = RA.v(F32, 32768, 8 * 96)
            rk3 = rk.rearrange("p (t c) -> p t c", c=96)
            if g == 1:
                dma("sp", rk3, ropeK_d.rearrange("(t p) c -> p t c", p=128), [], rkb, "rk")
            rt, rtb = RA.v(F32, 36864, 3 * 64)
            junk, junkb = RA.v(BF16, 43008, QL)
            wring = [0]

            def wslot(ncol_bytes=16384):
                sl = wring[0] % 2
                wring[0] += 1
                return sl

            blocks = [(OFF_Q + 256 * i, 256, 256 * i) for i in range(6)] + [(OFF_KPE, 64, 1536)]
            for hf in range(2):
                Z, Zb = RA.v(F32, 0, 4 * 1600)
                Z3 = Z.rearrange("p (t c) -> p t c", c=1600)
                for (c0, ncol, zo) in blocks:
                    sl = wslot()
                    w, wb = RW.v(BF16, sl * 16384, 32 * ncol)
                    w3 = w.rearrange("p (k c) -> p k c", c=ncol)
                    dma("pool", w3, win_d[:, c0:c0 + ncol].rearrange("(k p) c -> p k c", p=128), [], wb, f"RWa{sl}")
                    for tt in range(4):
                        gt = hf * 4 + tt
                        bk = next_bank()
                        mm_group(bank(bk)[:, 0:ncol], bk, [(hT3[:, k, gt * 128:(gt + 1) * 128], w3[:, k, :])
                                                          for k in range(32)], list(wb) + hTb)
                        copy_op("act", Z3[:, tt, zo:zo + ncol], bank(bk)[:, 0:ncol], [], [PB[bk]] + Zb)
                for tt in range(4):
                    gt = hf * 4 + tt
                    for (o, n, gv, gvb) in ((0, QL, gqa, gqab), (QL, KVL, gkv, gkvb)):
                        si = next_stat()
                        zz = Z3[:, tt, o:o + n]
                        P.op("act", lambda e, zz=zz, si=si, n=n: e.activation(junk[:, 0:n], zz, AF.Square,
                                                                            accum_out=stat[:, si:si + 1]),
                             reads=Zb, writes=junkb + [B_stat[si]])
                        rstd_ops(si, n)
                        P.op("dve", lambda e, zz=zz, si=si, gv=gv: e.scalar_tensor_tensor(
                            zz, zz, stat[:, si:si + 1], gv, ALU.mult, ALU.mult),
                            reads=[B_stat[si]] + gvb, writes=Zb)
                    for jb in range(2):
                        bk = next_bank()
                        tr_group(bk, [(bank(bk)[:, i * 128:(i + 1) * 128],
                                       Z3[:, tt, (jb * 4 + i) * 128:(jb * 4 + i + 1) * 128], ident[:, :])
                                      for i in range(4)], Zb)
                        copy_op("dve" if jb else "act", qnT3[:, jb * 4:(jb + 1) * 4, gt * 128:(gt + 1) * 128],
                                bank(bk).rearrange("p (k t) -> p k t", t=128), [], [PB[bk]] + qnTb)
                    if g == 0:
                        dma("sp", sckv_d[gt * 128:(gt + 1) * 128, :], Z3[:, tt, QL:QL + KVL], Zb, [], "Zst")
                        dma("sp", skpe_d[gt * 128:(gt + 1) * 128, :], Z3[:, tt, 1536:1600], Zb, [], "Zst")
                    bk = next_bank()
                    tr_group(bk, [(bank(bk)[:, i * 128:(i + 1) * 128],
                                   Z3[:, tt, QL + i * 128:QL + (i + 1) * 128], ident[:, :]) for i in range(4)], Zb)
                    copy_op("act", ckvT3[:, :, koff + gt * 128:koff + (gt + 1) * 128],
                            bank(bk).rearrange("p (k t) -> p k t", t=128), [], [PB[bk]] + ckvTb)
                    ksrc = Z3[:, tt, 1536:1600]
                    ksrcb = Zb
                    if g == 1:
                        zk = Z3[:, tt, 1536:1600]
                        P.op("dve", lambda e, zk=zk, gt=gt: e.tensor_tensor(rt[:, 0:64], zk, rk3[:, gt, 0:64], ALU.mult),
                             reads=list(Zb) + rkb, writes=rtb)

                        def f2(e, zk=zk, gt=gt):
                            e.tensor_tensor(rt[:, 64:96], zk[:, 32:64], rk3[:, gt, 64:96], ALU.mult)
                            return e.tensor_tensor(rt[:, 96:128], zk[:, 0:32], rk3[:, gt, 64:96], ALU.mult)
                        P.op("dve", f2, reads=list(Zb) + rkb, writes=rtb)

                        def f3(e):
                            e.tensor_tensor(rt[:, 128:160], rt[:, 0:32], rt[:, 64:96], ALU.subtract)
                            return e.tensor_tensor(rt[:, 160:192], rt[:, 32:64], rt[:, 96:128], ALU.add)
                        P.op("dve", f3, reads=rtb, writes=rtb)
                        ksrc = rt[:, 128:192]
                        ksrcb = rtb
                    bk = next_bank()
                    tr_group(bk, [(bank(bk)[0:64, 0:128], ksrc, ident[:, :])], ksrcb)
                    copy_op("dve", kpeT[:, koff + gt * 128:koff + (gt + 1) * 128], bank(bk)[0:64, 0:128], [],
                            [PB[bk]] + kpeTb)
            if g == 1:
                for kt in range(4):
                    cc, ccb = RA.v(F32, 36864 + 1024 + (kt % 2) * 2560, 576)
                    dma("sp", cc[:, 0:512], cckv_d[kt * 128:(kt + 1) * 128, :], [], ccb, f"cc{kt % 2}")
                    dma("sp", cc[:, 512:576], ckpe_d[kt * 128:(kt + 1) * 128, :], [], ccb, f"cc{kt % 2}")
                    bk = next_bank()
                    tr_group(bk, [(bank(bk)[:, i * 128:(i + 1) * 128], cc[:, i * 128:(i + 1) * 128], ident[:, :])
                                  for i in range(4)], ccb)
                    copy_op("act", ckvT3[:, :, kt * 128:(kt + 1) * 128],
                            bank(bk).rearrange("p (k t) -> p k t", t=128), [], [PB[bk]] + ckvTb)
                    bk = next_bank()
                    tr_group(bk, [(bank(bk)[0:64, 0:128], cc[:, 512:576], ident[:, :])], ccb)
                    copy_op("dve", kpeT[:, kt * 128:(kt + 1) * 128], bank(bk)[0:64, 0:128], [], [PB[bk]] + kpeTb)

            nseq, L = (4, 256) if g == 0 else (1, 1024)
            Lp = L + 16
            npad = nseq * Lp
            pooledT, pooledTb = RA.v(BF16, 0, 16 * NT)
            pooledT3 = pooledT.rearrange("p (j t) -> p j t", t=NT)
            invb, invbb = RA.v(F32, 32768, NT)
            U = [RA.v(F32, 43008 + i * 4352, npad) for i in range(2)]
            Wt = [RA.v(F32, 51712 + i * 4352, npad) for i in range(3)]
            for (a, b) in U + Wt:
                P.op("dve", lambda e, a=a: e.memset(a, 0.0), writes=b)
            for cb in range(8):
                sl = wslot()
                w, wb = RW.v(BF16, sl * 16384, 32 * 256)
                w3 = w.rearrange("p (k c) -> p k c", c=256)
                dma("pool", w3, win_d[:, cb * 256:(cb + 1) * 256].rearrange("(k p) c -> p k c", p=128), [], wb,
                    f"RWa{sl}")
                for ct in range(2):
                    j = cb * 2 + ct
                    gi = j // 4
                    if j % 4 == 0:
                        dma("sp", invb, invc_d[g:g + 1, gi * NT:(gi + 1) * NT].partition_broadcast(128)[:, 0, :],
                            [], invbb, "invb")
                    u, ub = U[j % 2]
                    u3 = u.rearrange("p (s c) -> p s c", c=Lp)
                    for tb in range(2):
                        bk = next_bank()
                        mm_group(bank(bk), bk, [(w3[:, k, ct * 128:(ct + 1) * 128], hT3[:, k, tb * 512:(tb + 1) * 512])
                                                for k in range(32)], list(wb) + hTb)
                        if g == 0:
                            o = u3[:, tb * 2:(tb + 1) * 2, 8:8 + 256]
                            i_ = bank(bk).rearrange("p (s c) -> p s c", c=256)
                        else:
                            o = u3[:, 0, 8 + tb * 512:8 + (tb + 1) * 512]
                            i_ = bank(bk)
                        copy_op("act", o, i_, [], [PB[bk]] + ub)
                    cur, curb = u3, ub
                    sh = [(1, 0, 1), (2, 1, 1), (4, 2, 2), (8, 4, 4)]
                    for wi in range(gi + 1):
                        lo, ls, rs = sh[wi]
                        hi = Lp - (0 if wi == 0 else (1 if wi == 1 else (3 if wi == 2 else 7)))
                        nx, nxb = Wt[wi % 3]
                        nx3 = nx.rearrange("p (s c) -> p s c", c=Lp)
                        if wi == 0:
                            a0, a1 = cur[:, :, 0:Lp - 1], cur[:, :, 1:Lp]
                            oo = nx3[:, :, 1:Lp]
                        else:
                            a0, a1 = cur[:, :, lo - ls:hi - ls], cur[:, :, lo + rs:hi + rs]
                            oo = nx3[:, :, lo:hi]
                        P.op("dve", lambda e, oo=oo, a0=a0, a1=a1: e.tensor_tensor(oo, a0, a1, ALU.add),
                             reads=curb, writes=nxb)
                        cur, curb = nx3, nxb
                    tm, tmb = Wt[(gi + 1) % 3]
                    tm3 = tm.rearrange("p (s c) -> p s c", c=Lp)
                    P.op("dve", lambda e, tm3=tm3, cur=cur: e.tensor_tensor(
                        tm3[:, :, 8:8 + L], cur[:, :, 8:8 + L], invb.rearrange("p (s c) -> p s c", c=L), ALU.mult),
                        reads=list(curb) + invbb, writes=tmb)
                    P.op("dve", lambda e, tm3=tm3, u3=u3, j=j: e.tensor_tensor(
                        pooledT3[:, j, :].rearrange("p (s c) -> p s c", c=L), tm3[:, :, 8:8 + L], u3[:, :, 8:8 + L],
                        ALU.subtract), reads=list(tmb) + ub, writes=pooledTb)
            sl = wslot()
            wpg, wpgb = RW.v(BF16, sl * 16384, 16 * 512)
            wpg3 = wpg.rearrange("p (k c) -> p k c", c=512)
            dma("pool", wpg3, wpg_d.rearrange("(k p) c -> p k c", p=128), [], wpgb, f"RWa{sl}")
            for gi in range(4):
                for dt in range(4):
                    jo = gi * 4 + dt
                    ms, msb = RA.v(BF16, 36864 + (jo % 2) * 2048, NT)
                    for tb in range(2):
                        bk = next_bank()
                        mm_group(bank(bk), bk, [(wpg3[:, gi * 4 + c, dt * 128:(dt + 1) * 128],
                                                 pooledT3[:, gi * 4 + c, tb * 512:(tb + 1) * 512]) for c in range(4)],
                                 list(wpgb) + pooledTb)
                        P.op("act", lambda e, ms=ms, bk=bk, tb=tb, jo=jo: e.activation(
                            ms[:, tb * 512:(tb + 1) * 512], bank(bk), AF.Copy, scale=pscT[:, jo:jo + 1]),
                            reads=[B_vecT], writes=[PB[bk]] + msb)
                    dma("sp", mix_d[jo], ms, msb, [B_mix[jo]], f"ms{jo % 2}")

            for cb in range(32):
                sl = wslot()
                w, wb = RW.v(BF16, sl * 16384, 32 * 256)
                w3 = w.rearrange("p (k c) -> p k c", c=256)
                c0 = OFF_GP + cb * 256
                dma("pool", w3, win_d[:, c0:c0 + 256].rearrange("(k p) c -> p k c", p=128), [], wb, f"RWa{sl}")
                for ct in range(2):
                    jg = cb * 2 + ct
                    gs, gsb = RA.v(F32, 43008 + (jg % 2) * 4096, NT)
                    for tb in range(2):
                        bk = next_bank()
                        mm_group(bank(bk), bk, [(w3[:, k, ct * 128:(ct + 1) * 128], hT3[:, k, tb * 512:(tb + 1) * 512])
                                                for k in range(32)], list(wb) + hTb)
                        P.op("act", lambda e, gs=gs, bk=bk, tb=tb: e.activation(
                            gs[:, tb * 512:(tb + 1) * 512], bank(bk), AF.Sigmoid), writes=[PB[bk]] + gsb)
                    dma("sp", sg_d[jg], gs, gsb, [B_sg[jg]], f"gs{jg % 2}")
            return qnT3, qnTb, ckvT3, ckvTb, kpeT, kpeTb

        def attention(g, qnT3, qnTb, ckvT3, ckvTb, kpeT, kpeTb):
            nkt = 8 if g == 0 else 12
            wkv, wkvb_ = RW.v(BF16, 0, 4 * 4096)
            wkv3 = wkv.rearrange("p (k c) -> p k c", c=4096)
            wkv4 = wkv.rearrange("p (k h c) -> p k h c", h=NH, c=256)
            dma("pool", wkv3, wkvb_d.rearrange("(k p) c -> p k c", p=128), [], wkvb_, "RWkv")
            vall, vallb = RH.v(BF16, 0, nkt * 2048)
            vall3 = vall.rearrange("p (k c) -> p k c", c=2048)
            oT, oTb = RA.v(BF16, 0, 16 * NT)
            oT3 = oT.rearrange("p (h t) -> p h t", t=NT)
            for kt in range(nkt):
                for hg in range(4):
                    bk = next_bank()
                    mm_group(bank(bk).rearrange("p (h c) -> p h c", c=128), bk,
                             [(ckvT3[:, k, kt * 128:(kt + 1) * 128], wkv4[:, k, hg * 4:(hg + 1) * 4, 128:256])
                              for k in range(4)], list(wkvb_) + ckvTb)
                    copy_op("act" if hg % 2 else "dve", vall3[:, kt, hg * 512:(hg + 1) * 512], bank(bk), [],
                            [PB[bk]] + vallb)
            if g == 1:
                cosT, cosTb = RA.v(F32, 47104, NT, parts=64)
                sinT, sinTb = RA.v(F32, 51200, NT, parts=64)
                dma("sp", cosT, ropeT_d[0], [], cosTb, "cosT")
                dma("sp", sinT, ropeT_d[1], [], sinTb, "sinT")
            qblocks = [(s * 256, 256, [2 * s, 2 * s + 1]) for s in range(4)] if g == 0 else \
                      [(0, 512, list(range(12))), (512, 512, list(range(12)))]
            qbi = 0
            for h in range(NH):
                hs = h % 2
                wq, wqb_ = RH.v(BF16, 49152 + hs * 3072, 8 * 192)
                wq3 = wq.rearrange("p (k c) -> p k c", c=192)
                dma("pool", wq3, wqb_d[:, h * 192:(h + 1) * 192].rearrange("(k p) c -> p k c", p=128), [], wqb_,
                    f"wq{hs}")
                QN, QNb = RA.v(BF16, 32768 + hs * 2048, NT)
                QP, QPb = RA.v(BF16, 36864 + hs * 2048, NT, parts=64)
                KN, KNb = RA.v(BF16, 40960 + hs * 3072, 1536)
                if g == 1:
                    wr, wrb = RH.v(BF16, 55296 + hs * 1024, 8 * 64)
                    wr3 = wr.rearrange("p (k c) -> p k c", c=64)
                    c0 = h * 192 + 128
                    dma("pool", wr3[:, :, 0:32], wqb_d[:, c0 + 32:c0 + 64].rearrange("(k p) c -> p k c", p=128), [],
                        wrb, f"wr{hs}")
                    dma("pool", wr3[:, :, 32:64], wqb_d[:, c0:c0 + 32].rearrange("(k p) c -> p k c", p=128), [],
                        wrb, f"wr{hs}")
                for tb in range(2):
                    bk = next_bank()
                    mm_group(bank(bk), bk, [(wq3[:, k, 0:128], qnT3[:, k, tb * 512:(tb + 1) * 512]) for k in range(8)],
                             list(wqb_) + qnTb)
                    copy_op("act", QN[:, tb * 512:(tb + 1) * 512], bank(bk), [], [PB[bk]] + QNb)
                    bk = next_bank()
                    mm_group(bank(bk)[0:64, :], bk, [(wq3[:, k, 128:192], qnT3[:, k, tb * 512:(tb + 1) * 512])
                                                     for k in range(8)], list(wqb_) + qnTb)
                    if g == 0:
                        copy_op("dve", QP[:, tb * 512:(tb + 1) * 512], bank(bk)[0:64, :], [], [PB[bk]] + QPb)
                    else:
                        bk2 = next_bank()
                        mm_group(bank(bk2)[0:64, :], bk2, [(wr3[:, k, :], qnT3[:, k, tb * 512:(tb + 1) * 512])
                                                           for k in range(8)], list(wrb) + qnTb)
                        t1, t1b = RA.v(F32, 55296 + tb * 4096, 512, parts=64)
                        t2, t2b = RA.v(F32, 57344 + tb * 4096, 512, parts=64)
                        P.op("dve", lambda e, t1=t1, bk=bk, tb=tb: e.tensor_tensor(
                            t1, bank(bk)[0:64, :], cosT[:, tb * 512:(tb + 1) * 512], ALU.mult),
                            reads=cosTb, writes=[PB[bk]] + t1b)
                        P.op("dve", lambda e, t2=t2, bk2=bk2, tb=tb: e.tensor_tensor(
                            t2, bank(bk2)[0:64, :], sinT[:, tb * 512:(tb + 1) * 512], ALU.mult),
                            reads=sinTb, writes=[PB[bk2]] + t2b)
                        P.op("dve", lambda e, t1=t1, t2=t2, QP=QP, tb=tb: e.tensor_tensor(
                            QP[:, tb * 512:(tb + 1) * 512], t1, t2, ALU.add), reads=list(t1b) + list(t2b), writes=QPb)
                for kb in range(nkt // 4):
                    bk = next_bank()
                    mm_group(bank(bk), bk, [(wkv3[:, k, h * 256:h * 256 + 128], ckvT3[:, k, kb * 512:(kb + 1) * 512])
                                            for k in range(4)], list(wkvb_) + ckvTb)
                    copy_op("dve" if kb % 2 else "act", KN[:, kb * 512:(kb + 1) * 512], bank(bk), [], [PB[bk]] + KNb)
                for (q0, nq, kts) in qblocks:
                    ob, sb_ = 4 + qbi % 2, 6 + qbi % 2
                    qbi += 1
                    pend = []
                    nk = len(kts)

                    def emit_pv(item, nq=nq, ob=ob, sb_=sb_, h=h, nk=nk):
                        pi, pkt, pPT, pPTb = item

                        def fpv(e):
                            e.matmul(bank(ob)[:, 0:nq], vall3[:, pkt, h * 128:(h + 1) * 128], pPT[:, 0:nq],
                                     start=(pi == 0), stop=(pi == nk - 1))
                            return e.matmul(bank(sb_)[:, 0:nq], onesb[:, :], pPT[:, 0:nq],
                                            start=(pi == 0), stop=(pi == nk - 1))
                        P.op("pe", fpv, reads=list(pPTb) + vallb + [B_onesb], writes=[PB[ob], PB[sb_]])

                    for i, kt in enumerate(kts):
                        sbk = i % 4
                        PT, PTb = RH.v(BF16, 57344 + (i % 4) * 1024, 512)

                        def fs(e, sbk=sbk, kt=kt, q0=q0, nq=nq, KN=KN, QN=QN, QP=QP):
                            e.matmul(bank(sbk)[:, 0:nq], KN[:, kt * 128:(kt + 1) * 128], QN[:, q0:q0 + nq],
                                     start=True, stop=False)
                            return e.matmul(bank(sbk)[:, 0:nq], kpeT[:, kt * 128:(kt + 1) * 128], QP[:, q0:q0 + nq],
                                            start=False, stop=True)
                        P.op("pe", fs, reads=list(KNb) + list(QNb) + list(QPb) + kpeTb, writes=[PB[sbk]])
                        P.op("act", lambda e, PT=PT, sbk=sbk, nq=nq: e.activation(
                            PT[:, 0:nq], bank(sbk)[:, 0:nq], AF.Exp, scale=float(SCALE)),
                            writes=[PB[sbk]] + PTb)
                        pend.append((i, kt, PT, PTb))
                        if len(pend) > 2:
                            emit_pv(pend.pop(0))
                    while pend:
                        emit_pv(pend.pop(0))
                    ri, rib = RH.v(F32, 61440, 512)
                    P.op("dve", lambda e, ri=ri, sb_=sb_, nq=nq: e.reciprocal(ri[:, 0:nq], bank(sb_)[:, 0:nq]),
                         writes=[PB[sb_]] + rib)
                    P.op("dve", lambda e, ri=ri, ob=ob, nq=nq, h=h, q0=q0: e.tensor_tensor(
                        oT3[:, h, q0:q0 + nq], bank(ob)[:, 0:nq], ri[:, 0:nq], ALU.mult),
                        reads=rib, writes=[PB[ob]] + oTb)
            return oT3, oTb

        def comb_wo(g, oT3, oTb):
            mixT, mixTb = RA.v(BF16, 32768, 16 * NT)
            mixT3 = mixT.rearrange("p (k t) -> p k t", t=NT)
            dma("sp", mixT3, mix_d.rearrange("k p t -> p k t"), B_mix, mixTb, "mixT")
            cT, cTb = RH.v(BF16, 0, 32 * NT)
            cT3 = cT.rearrange("p (k t) -> p k t", t=NT)
            for cb in range(16):
                sl = cb % 2
                wp, wpb = RW.v(BF16, sl * 16384, 16 * 256)
                wa, wab = RW.v(BF16, sl * 16384 + 8192, 16 * 256)
                wp3 = wp.rearrange("p (k c) -> p k c", c=256)
                wa3 = wa.rearrange("p (k c) -> p k c", c=256)
                dma("pool", wp3, wpo_d[:, cb * 256:(cb + 1) * 256].rearrange("(k p) c -> p k c", p=128), [], wpb,
                    f"RWa{sl}")
                dma("pool", wa3, wao_d[:, cb * 256:(cb + 1) * 256].rearrange("(k p) c -> p k c", p=128), [], wab,
                    f"RWb{sl}")
                for ct in range(2):
                    j = cb * 2 + ct
                    gp, gpb = RW.v(F32, 32768 + (j % 2) * 8192, NT)
                    ga, gab = RW.v(F32, 32768 + (j % 2) * 8192 + 4096, NT)
                    dma("sp", gp, sg_d[j], [B_sg[j]], gpb, f"gp{j % 2}")
                    dma("sp", ga, sg_d[32 + j], [B_sg[32 + j]], gab, f"ga{j % 2}")
                    for tb in range(2):
                        b1, b2 = next_bank(), next_bank()
                        mm_group(bank(b1), b1, [(wp3[:, k, ct * 128:(ct + 1) * 128], mixT3[:, k, tb * 512:(tb + 1) * 512])
                                                for k in range(16)], list(wpb) + mixTb)
                        mm_group(bank(b2), b2, [(wa3[:, k, ct * 128:(ct + 1) * 128], oT3[:, k, tb * 512:(tb + 1) * 512])
                                                for k in range(16)], list(wab) + oTb)
                        t1, t1b = RW.v(F32, 49152 + tb * 4096, 512)
                        t2, t2b = RW.v(F32, 51200 + tb * 4096, 512)
                        P.op("dve", lambda e, t1=t1, b1=b1, gp=gp, tb=tb: e.tensor_tensor(
                            t1, bank(b1), gp[:, tb * 512:(tb + 1) * 512], ALU.mult), reads=gpb, writes=[PB[b1]] + t1b)
                        P.op("dve", lambda e, t2=t2, b2=b2, ga=ga, tb=tb: e.tensor_tensor(
                            t2, bank(b2), ga[:, tb * 512:(tb + 1) * 512], ALU.mult), reads=gab, writes=[PB[b2]] + t2b)
                        P.op("dve", lambda e, t1=t1, t2=t2, j=j, tb=tb: e.tensor_tensor(
                            cT3[:, j, tb * 512:(tb + 1) * 512], t1, t2, ALU.add),
                            reads=list(t1b) + list(t2b), writes=cTb)
            Gb, Gbb = build_Gb(g, 1, RA, 49152, None, None)
            for q in range(4):
                Y, Yb = RA.v(F32, 0, 2 * D)
                Y3 = Y.rearrange("p (t n) -> p t n", n=D)
                for nb in range(8):
                    sl = (q * 8 + nb) % 2
                    w, wb = RW.v(BF16, sl * 32768, 32 * 512)
                    w3 = w.rearrange("p (k c) -> p k c", c=512)
                    dma("pool", w3, wo_d[:, nb * 512:(nb + 1) * 512].rearrange("(k p) c -> p k c", p=128), [], wb,
                        f"RW{sl}")
                    for tt in range(2):
                        gt = q * 2 + tt
                        bk = next_bank()
                        mm_group(bank(bk), bk, [(cT3[:, k, gt * 128:(gt + 1) * 128], w3[:, k, :]) for k in range(32)],
                                 list(wb) + cTb)
                        copy_op("act", Y3[:, tt, nb * 512:(nb + 1) * 512], bank(bk), [],
                                [PB[bk]] + Yb[tt * 16 + nb * 2:tt * 16 + nb * 2 + 2])
                for tt in range(2):
                    gt = q * 2 + tt
                    xt, xtb = RA.v(F32, 32768, D)
                    post_tile(g, 1, Y3[:, tt, :], Yb[tt * 16:(tt + 1) * 16], Gb, Gbb, xt, xtb, "RAxw",
                              x1_d, [B_x1[g][gt]], x2_d, [B_x2[g][gt]], g * NT + gt * 128)

        stage0()
        for g in range(2):
            hT3, hTb = pre(g, 0, x_d, None)
            ffn_a(0, hT3, hTb)
            ffn_b(0, g, 0, x_d, None, x1_d, B_x1)
            if stop_after == "ffn1":
                continue
            hT3, hTb = pre(g, 1, x1_d, B_x1)
            mx = mixer1(g, hT3, hTb)
            if stop_after == "mixer1":
                continue
            oT3, oTb = attention(g, *mx)
            comb_wo(g, oT3, oTb)
            if stop_after == "mixer":
                continue
            hT3, hTb = pre(g, 2, x2_d, B_x2)
            ffn_a(1, hT3, hTb)
            ffn_b(1, g, 2, x2_d, B_x2, y_d, None)
        P.emit(nc, st)
    return nc


def _consts():
    ident = np.eye(128, dtype=np.float32)
    t = np.arange(NT)
    row = (t // 64).astype(np.float32)
    col = (t % 64).astype(np.float32)
    inv = (10000.0 ** (-np.arange(16, dtype=np.float32) / 16)).astype(np.float32)
    ang = np.concatenate([row[:, None] * inv, col[:, None] * inv], axis=-1).astype(np.float32)
    cos, sin = np.cos(ang).astype(np.float32), np.sin(ang).astype(np.float32)
    ropeT = np.stack([np.concatenate([cos.T, cos.T], 0), np.concatenate([-sin.T, sin.T], 0)]).astype(np.float32)
    ropeK = np.concatenate([cos, cos, sin], axis=1).astype(np.float32)
    invc = np.zeros((2, 4, NT), np.float32)
    for g, L in enumerate((256, 1024)):
        tt = np.arange(NT) % L
        for wi, w in enumerate(WINS):
            lo = np.clip(tt - w // 2, 0, L)
            hi = np.clip(tt + w // 2, 0, L)
            invc[g, wi] = 1.0 / (hi - lo)
    return ident, ropeT, ropeK, invc.reshape(2, 4 * NT)


def make_in_maps(inputs):
    f = lambda a: np.ascontiguousarray(np.asarray(a, dtype=np.float32))
    ident, ropeT, ropeK, invc = _consts()
    shared = {
        "w_mod": f(inputs["w_mod"][0]),
        "b_mod": f(inputs["b_mod"][0]).reshape(288, 128),
        "g_pre": f(inputs["g_pre"][0]).reshape(96, 128),
        "g_post": f(inputs["g_post"][0]).reshape(96, 128),
        "w_f1i": f(inputs["w_ffn1_in"][0]), "w_f1o": f(inputs["w_ffn1_out"][0]),
        "w_f2i": f(inputs["w_ffn2_in"][0]), "w_f2o": f(inputs["w_ffn2_out"][0]),
        "w_in": f(inputs["w_in"][0]),
        "g_qa": f(inputs["g_qa"][0]).reshape(1, QL),
        "w_qb": f(inputs["w_qb"][0]),
        "g_kva": f(inputs["g_kva"][0]).reshape(1, KVL),
        "w_kvb": f(inputs["w_kvb"][0]),
        "w_pg": f(inputs["w_pool_grp"][0]).reshape(2048, 512),
        "pool_scale": f(inputs["pool_scale"][0]).reshape(16, 128),
        "w_po": f(inputs["w_pool_out"][0]), "w_ao": f(inputs["w_attn_out"][0]), "w_o": f(inputs["w_o"][0]),
        "ident": ident, "ropeT": ropeT, "ropeK": ropeK, "invc": invc,
    }
    xp = f(inputs["x_prompt"])
    xs = f(inputs["x_sample"])
    maps = []
    for c in range(8):
        m = dict(shared)
        m["x"] = np.concatenate([xp[4 * c:4 * c + 4].reshape(NT, D), xs[c]], axis=0)
        m["cckv"] = f(inputs["cache_ckv"][c, 0])
        m["ckpe"] = f(inputs["cache_kpe"][c, 0])
        m["cvec"] = np.stack([f(inputs["c_ctx"]), f(inputs["c"][c])], axis=0)
        maps.append(m)
    return maps


_NC_CACHE = {}


def kernel(**inputs):
    if "nc" not in _NC_CACHE:
        _NC_CACHE["nc"] = build_nc()
    nc = _NC_CACHE["nc"]
    maps = make_in_maps(inputs)
    res = run_bass_kernel_spmd(nc, maps, core_ids=list(range(8)))
    r = res.results
    y = [np.asarray(r[c]["y"]) for c in range(8)]
    y_prompt = np.concatenate([y[c][:NT].reshape(4, 256, D) for c in range(8)], axis=0)
    y_sample = np.stack([y[c][NT:] for c in range(8)], axis=0)
    s_ckv = np.concatenate([np.asarray(r[c]["sckv"]).reshape(4, 1, 256, KVL) for c in range(8)], axis=0)
    s_kpe = np.concatenate([np.asarray(r[c]["skpe"]).reshape(4, 1, 256, ROPE) for c in range(8)], axis=0)
    return (y_prompt.astype(np.float32), y_sample.astype(np.float32),
            s_ckv.astype(np.float32), s_kpe.astype(np.float32))
```

```python
import math
from contextlib import ExitStack

import numpy as np

import concourse.bass as bass
import concourse.mybir as mybir
from concourse.bass_utils import run_bass_kernel_spmd

F32 = mybir.dt.float32
BF16 = mybir.dt.bfloat16
ALU = mybir.AluOpType
AF = mybir.ActivationFunctionType

D = 4096
DFF = 11008
NT = 1024
NMOD = 9
EPS = 1e-6
QL, KVL, ROPE = 1024, 512, 64
NH, DN, DV = 16, 128, 128
QKD = DN + ROPE
OFF_Q = 2048
OFF_KV = OFF_Q + QL
OFF_KPE = OFF_KV + KVL
OFF_GP = OFF_KPE + ROPE
OFF_GA = OFF_GP + D
INC = OFF_GA + D
SCALE = QKD ** -0.5
PAST = 512
WINS = (2, 4, 8, 16)


class Buf:
    __slots__ = ("name", "w", "r")

    def __init__(self, name):
        self.name = name
        self.w = None
        self.r = {}


class Op:
    __slots__ = ("eng", "fn", "reads", "writes", "ndma", "chan", "deps", "sig", "waits")


class Prog:
    ENGS = ("pe", "act", "dve", "pool", "sp")
    BLK = {"pe": "tensor", "act": "scalar", "dve": "vector", "pool": "gpsimd", "sp": "sync"}
    ROT = 20000

    def __init__(self):
        self.ops = []

    def op(self, eng, fn, reads=(), writes=(), ndma=0, chan=None):
        o = Op()
        o.eng, o.fn, o.reads, o.writes, o.ndma, o.chan = eng, fn, tuple(reads), tuple(writes), ndma, chan
        o.sig = None
        o.waits = ()
        if ndma:
            assert chan is not None
        self.ops.append(o)
        return o

    def plan(self):
        ops = self.ops
        needs = set()
        for i, o in enumerate(ops):
            deps = set()
            for b in o.reads:
                if b.w is not None:
                    deps.add(b.w)
            for b in o.writes:
                if b.w is not None:
                    deps.add(b.w)
                deps.update(b.r.values())
            deps.discard(i)
            if o.eng == "pe" and not o.ndma:
                deps = {d for d in deps if not (ops[d].eng == "pe" and not ops[d].ndma)}
            o.deps = deps
            needs.update(deps)
            key = ("dma", o.chan) if o.ndma else o.eng
            for b in o.reads:
                b.r[key] = i
            for b in o.writes:
                b.w = i
                b.r = {}
        eng_cnt = {e: 0 for e in self.ENGS}
        eng_gen = {e: 0 for e in self.ENGS}
        chan_cnt = {}
        self.sem_keys = []
        seen = set()

        def reg(k):
            if k not in seen:
                seen.add(k)
                self.sem_keys.append(k)

        for i, o in enumerate(ops):
            if o.ndma:
                c = chan_cnt.get(o.chan, 0) + 16 * o.ndma
                chan_cnt[o.chan] = c
                o.sig = (("dma", o.chan), c)
                reg(o.sig[0])
            elif i in needs:
                if eng_cnt[o.eng] >= self.ROT:
                    eng_cnt[o.eng] = 0
                    eng_gen[o.eng] += 1
                eng_cnt[o.eng] += 1
                o.sig = ((o.eng, eng_gen[o.eng]), eng_cnt[o.eng])
                reg(o.sig[0])
        self.final_dma = dict(chan_cnt)
        waited = {e: {} for e in self.ENGS}
        for o in ops:
            w = {}
            for d in o.deps:
                s, v = ops[d].sig
                if w.get(s, 0) < v:
                    w[s] = v
            wl = []
            we = waited[o.eng]
            for s, v in w.items():
                if we.get(s, 0) < v:
                    we[s] = v
                    wl.append((s, v))
            o.waits = wl

    def emit(self, nc, st):
        self.plan()
        sems = {}
        for n, k in enumerate(self.sem_keys):
            sems[k] = st.enter_context(nc.semaphore(f"s{n}"))
        per = {e: [o for o in self.ops if o.eng == e] for e in self.ENGS}
        with nc.Block() as block:
            for eng in self.ENGS:
                lst = per[eng]

                def body(e, lst=lst, eng=eng):
                    for o in lst:
                        for (s, v) in o.waits:
                            e.wait_ge(sems[s], v)
                        if o.ndma:
                            o.fn(e, sems[o.sig[0]])
                        else:
                            inst = o.fn(e)
                            if o.sig is not None:
                                inst.then_inc(sems[o.sig[0]], 1)
                    if eng == "sp":
                        for ch, c in self.final_dma.items():
                            e.wait_ge(sems[("dma", ch)], c)

                getattr(block, self.BLK[eng])(body)


class Region:
    def __init__(self, nc, st, name, nbytes):
        self.t = st.enter_context(nc.sbuf_tensor(name, [128, nbytes // 2], BF16))
        self.name = name
        self.bufs = [Buf(f"{name}.{i}") for i in range(nbytes // 1024)]

    def v(self, dtype, off, n, parts=128):
        es = 4 if dtype is F32 else 2
        assert off % 4 == 0 and off + n * es <= len(self.bufs) * 1024, (self.name, off, n)
        a = self.t[0:parts, off // 2: off // 2 + n * es // 2]
        if dtype is F32:
            a = a.bitcast(F32)
        return a, self.bufs[off // 1024: (off + n * es + 1023) // 1024]


def build_nc(stop_after=None, dbg=False):
    nc = bass.Bass("TRN2", target_bir_lowering=False)
    P = Prog()

    def din(name, shape, dt=F32):
        return nc.dram_tensor(name, list(shape), dt, kind="ExternalInput").ap()

    def dscr(name, shape, dt=F32, out=False):
        return nc.dram_tensor(name, list(shape), dt, kind="ExternalOutput" if out else "Internal").ap()

    x_d = din("x", [2 * NT, D])
    cckv_d = din("cckv", [PAST, KVL])
    ckpe_d = din("ckpe", [PAST, ROPE])
    cvec_d = din("cvec", [2, D])
    wmod_d = din("w_mod", [D, NMOD * D])
    bmod_d = din("b_mod", [288, 128])
    gpre_d = din("g_pre", [96, 128])
    gpost_d = din("g_post", [96, 128])
    wfi_d = [din("w_f1i", [D, 2 * DFF]), din("w_f2i", [D, 2 * DFF])]
    wfo_d = [din("w_f1o", [DFF, D]), din("w_f2o", [DFF, D])]
    win_d = din("w_in", [D, INC])
    gqa_d = din("g_qa", [1, QL])
    wqb_d = din("w_qb", [QL, NH * QKD])
    gkva_d = din("g_kva", [1, KVL])
    wkvb_d = din("w_kvb", [KVL, NH * (DN + DV)])
    wpg_d = din("w_pg", [2048, 512])
    psc_d = din("pool_scale", [16, 128])
    wpo_d = din("w_po", [2048, D])
    wao_d = din("w_ao", [2048, D])
    wo_d = din("w_o", [D, D])
    ident_d = din("ident", [128, 128])
    ropeT_d = din("ropeT", [2, 64, NT])
    ropeK_d = din("ropeK", [NT, 96])
    invc_d = din("invc", [2, 4 * NT])

    y_d = dscr("y", [2 * NT, D], out=True)
    sckv_d = dscr("sckv", [NT, KVL], out=True)
    skpe_d = dscr("skpe", [NT, ROPE], out=True)
    x1_d = dscr("x1s", [2 * NT, D], out=dbg)
    x2_d = dscr("x2s", [2 * NT, D], out=dbg)
    actT_d = dscr("actT", [86, 128, NT], BF16)
    sg_d = dscr("sgs", [64, 128, NT])
    mix_d = dscr("mixs", [16, 128, NT], BF16, out=False)

    B_x1 = [[Buf(f"x1.{g}.{t}") for t in range(8)] for g in range(2)]
    B_x2 = [[Buf(f"x2.{g}.{t}") for t in range(8)] for g in range(2)]
    B_actT = [Buf(f"actT.{i}") for i in range(43)]
    B_sg = [Buf(f"sg.{i}") for i in range(64)]
    B_mix = [Buf(f"mix.{i}") for i in range(16)]

    with ExitStack() as st:
        RH = Region(nc, st, "RH", 64 * 1024)
        RW = Region(nc, st, "RW", 64 * 1024)
        RA = Region(nc, st, "RA", 64 * 1024)
        ps = st.enter_context(nc.psum_tensor("ps", [128, 4096], F32))
        PB = [Buf(f"pb{i}") for i in range(8)]

        def bank(i):
            return ps[:, i * 512:(i + 1) * 512]

        bank_ctr = [0]
        bank_all = [False]

        def next_bank():
            b = bank_ctr[0] % (8 if bank_all[0] else 7)
            bank_ctr[0] += 1
            return b

        def ctile(name, shape, dt=F32):
            return st.enter_context(nc.sbuf_tensor("c_" + name, list(shape), dt)), Buf(name)

        ident, B_ident = ctile("ident", [128, 128])
        onesf, B_onesf = ctile("onesf", [128, 128])
        onesb, B_onesb = ctile("onesb", [128, 128], BF16)
        modT, B_modT = ctile("modT", [128, 288, 2])
        vecT, B_vecT = ctile("vecT", [128, 512])
        ABG, B_ABG = ctile("ABG", [128, 2 * 3 * 2 * 32])
        sT, B_sT = ctile("sT", [128, 64], BF16)
        stat, _ = ctile("stat", [128, 64])
        B_stat = [Buf(f"stat{i}") for i in range(64)]
        stat_ctr = [0]

        def next_stat():
            i = stat_ctr[0] % 64
            stat_ctr[0] += 1
            return i

        bmT = vecT[:, 0:288]
        gpreT = vecT[:, 288:384]
        gpostT = vecT[:, 384:480]
        pscT = vecT[:, 480:496]

        def A_ap(g, k):
            o = ((g * 3 + k) * 2) * 32
            return ABG[:, o:o + 32]

        def G_ap(g, k):
            o = ((g * 3 + k) * 2 + 1) * 32
            return ABG[:, o:o + 32]

        def Bv_ap(g, k, j):
            return modT[:, (3 * k) * 32 + j, g:g + 1]

        def dma(eng, out_ap, in_ap, reads, writes, chan):
            def fn(e, sem):
                e.dma_start(out=out_ap, in_=in_ap).then_inc(sem, 16)
            P.op(eng, fn, reads=reads, writes=writes, ndma=1, chan=chan)

        def mm_group(out_ap, bk, pairs, reads):
            def fn(e):
                n = len(pairs)
                inst = None
                for i, (l, r) in enumerate(pairs):
                    inst = e.matmul(out_ap, l, r, start=(i == 0), stop=(i == n - 1))
                return inst
            P.op("pe", fn, reads=reads, writes=[PB[bk]])

        def tr_group(bk, items, reads):
            def fn(e):
                inst = None
                for o, i, idn in items:
                    inst = e.transpose(o, i, idn)
                return inst
            P.op("pe", fn, reads=list(reads) + [B_ident], writes=[PB[bk]])

        def rstd_ops(ss_i, n):
            a = stat[:, ss_i:ss_i + 1]

            P.op("act", lambda e: e.activation(a, a, AF.Sqrt, bias=float(EPS), scale=1.0 / n),
                 reads=[], writes=[B_stat[ss_i]])
            P.op("dve", lambda e: e.reciprocal(a, a), reads=[], writes=[B_stat[ss_i]])

        mrow, B_mrow = ctile("mrow", [2, 512])

        def mod_rows(w3, wb, k0, nk, first, last):
            sT3 = sT[:, :].rearrange("p (k g) -> p k g", g=2)

            def fn(e):
                inst = None
                for k in range(nk):
                    inst = e.matmul(bank(7)[0:2, :], sT3[:, k0 + k, :], w3[:, k, :],
                                    start=(first and k == 0), stop=(last and k == nk - 1), skip_group_check=True)
                return inst
            P.op("pe", fn, reads=list(wb) + [B_sT], writes=[PB[7]])

        def mod_finish(cb):
            P.op("act", lambda e: e.activation(mrow[:, :], bank(7)[0:2, :], AF.Copy), writes=[PB[7], B_mrow])
            bk = next_bank()
            tr_group(bk, [(bank(bk)[:, 2 * ct:2 * ct + 2], mrow[0:2, ct * 128:(ct + 1) * 128], ident[0:2, 0:2])
                          for ct in range(4)], [B_mrow])

            def ev(e):
                return e.tensor_tensor(
                    modT[:, cb * 4:(cb + 1) * 4, :],
                    bank(bk)[:, 0:8].rearrange("p (c g) -> p c g", g=2),
                    bmT[:, cb * 4:(cb + 1) * 4].unsqueeze(2).to_broadcast([128, 4, 2]),
                    ALU.add)
            P.op("dve", ev, reads=[B_vecT], writes=[PB[bk], B_modT])

        def stage0():
            dma("sp", ident[:, :], ident_d[:, :], [], [B_ident], "ident")
            P.op("dve", lambda e: e.memset(onesf[:, :], 1.0), writes=[B_onesf])
            P.op("dve", lambda e: e.memset(onesb[:, :], 1.0), writes=[B_onesb])
            c2, c2b = RA.v(F32, 0, D, parts=2)
            dma("sp", c2, cvec_d[:, :], [], c2b, "c2")
            P.op("act", lambda e: e.activation(c2, c2, AF.Silu), writes=c2b)
            bk = next_bank()
            tr_group(bk, [(bank(bk)[:, 2 * i:2 * i + 2], c2[:, i * 128:(i + 1) * 128], ident[0:2, 0:2])
                          for i in range(32)], c2b)
            P.op("dve", lambda e, bk=bk: e.tensor_copy(sT[:, :], bank(bk)[:, 0:64]), writes=[PB[bk], B_sT])
            tmp, tmpb = RA.v(F32, 16384, 6 * 128)
            srcs = [(bmod_d[0:96, :], 96), (bmod_d[96:192, :], 96), (bmod_d[192:288, :], 96),
                    (gpre_d[:, :], 96), (gpost_d[:, :], 96), (psc_d[:, :], 16)]
            for i, (s, n) in enumerate(srcs):
                dma("sp", tmp[0:n, i * 128:(i + 1) * 128], s, [], tmpb, "tmpv")
            bk2 = next_bank()
            items = []
            col = 0
            for i, (s, n) in enumerate(srcs):
                items.append((bank(bk2)[:, col:col + n], tmp[0:n, i * 128:(i + 1) * 128], ident[0:n, 0:n]))
                col += n
            tr_group(bk2, items, tmpb)
            P.op("dve", lambda e: e.tensor_copy(vecT[:, 0:496], bank(bk2)[:, 0:496]), writes=[PB[bk2], B_vecT])
            sT3 = sT[:, :].rearrange("p (k g) -> p k g", g=2)
            for cb in range(16):
                slot = cb % 2
                w, wb = RW.v(BF16, slot * 32768, 32 * 512)
                w3 = w.rearrange("p (k c) -> p k c", c=512)
                dma("pool", w3, wmod_d[:, cb * 512:(cb + 1) * 512].rearrange("(k p) c -> p k c", p=128),
                    [], wb, f"RW{slot}")
                mod_rows(w3, wb, 0, 32, True, True)
                mod_finish(cb)
            for g in range(2):
                abg_A(g, 0)

        def abg_A(g, k):
            def fa(e):
                return e.scalar_tensor_tensor(A_ap(g, k), modT[:, (3 * k + 1) * 32:(3 * k + 2) * 32, g], 1.0,
                                              gpreT[:, k * 32:(k + 1) * 32], ALU.add, ALU.mult)
            P.op("dve", fa, reads=[B_modT, B_vecT], writes=[B_ABG])

        def abg_G(g, k):
            ck = 1.0 if k == 1 else 0.5

            def fg(e):
                return e.scalar_tensor_tensor(G_ap(g, k), modT[:, (3 * k + 2) * 32:(3 * k + 3) * 32, g], ck,
                                              gpostT[:, k * 32:(k + 1) * 32], ALU.mult, ALU.mult)
            P.op("dve", fg, reads=[B_modT, B_vecT], writes=[B_ABG])

        mod_pending = [(cb, kh) for cb in range(16, 72) for kh in range(2)]
        mod_state = {"n": 0, "done": False}

        def mod_more(n):
            sT3 = sT[:, :].rearrange("p (k g) -> p k g", g=2)
            for _ in range(n):
                if not mod_pending:
                    break
                cb, kh = mod_pending.pop(0)
                slot = mod_state["n"] % 2
                mod_state["n"] += 1
                w, wb = RA.v(BF16, slot * 16384, 16 * 512)
                w3 = w.rearrange("p (k c) -> p k c", c=512)
                dma("pool", w3, wmod_d[kh * 2048:(kh + 1) * 2048, cb * 512:(cb + 1) * 512]
                    .rearrange("(k p) c -> p k c", p=128), [], wb, f"RAm{slot}")

                mod_rows(w3, wb, kh * 16, 16, kh == 0, kh == 1)
                if kh == 1:
                    mod_finish(cb)
            if not mod_pending and not mod_state["done"]:
                mod_state["done"] = True
                bank_all[0] = True
                for g in range(2):
                    for k in range(3):
                        if k > 0:
                            abg_A(g, k)
                        abg_G(g, k)

        def build_Gb(g, k, reg, off, scratch_reg, scratch_off):
            Gb, Gbb = reg.v(F32, off, D)
            if scratch_reg is None:
                dg, dgb = dgt[:, :], [B_dgt]
            else:
                dg, dgb = scratch_reg.v(F32, scratch_off, 512)
            for jb in range(8):
                def fd(e, jb=jb):
                    inst = None
                    for i in range(4):
                        j = jb * 4 + i
                        inst = e.tensor_scalar(dg[:, i * 128:(i + 1) * 128], ident[:, :], G_ap(g, k)[:, j:j + 1],
                                               None, ALU.mult)
                    return inst
                P.op("dve", fd, reads=[B_ident, B_ABG], writes=dgb)
                bk = next_bank()

                def fm(e, bk=bk):
                    inst = None
                    for i in range(4):
                        inst = e.matmul(bank(bk)[:, i * 128:(i + 1) * 128], onesf[:, :], dg[:, i * 128:(i + 1) * 128],
                                        start=True, stop=True)
                    return inst
                P.op("pe", fm, reads=list(dgb) + [B_onesf], writes=[PB[bk]])
                P.op("act", lambda e, bk=bk, jb=jb: e.activation(Gb[:, jb * 512:(jb + 1) * 512], bank(bk), AF.Copy),
                     writes=[PB[bk]] + Gbb[jb * 2:jb * 2 + 2])
            return Gb, Gbb

        def pre(g, k, src_d, src_bufs):
            hT, hTb = RH.v(BF16, 0, 32 * NT)
            hT3 = hT.rearrange("p (j t) -> p j t", t=NT)
            junk, junkb = RA.v(BF16, 32768, D)
            for tt in range(8):
                xt, xtb = RA.v(F32, (tt % 2) * 16384, D)
                r0 = g * NT + tt * 128
                dma("sp", xt, src_d[r0:r0 + 128, :], [src_bufs[g][tt]] if src_bufs else [], xtb, f"RAx{tt % 2}")
                si = next_stat()
                P.op("act", lambda e, xt=xt, si=si: e.activation(junk, xt, AF.Square, accum_out=stat[:, si:si + 1]),
                     reads=xtb, writes=junkb + [B_stat[si]])
                rstd_ops(si, D)
                P.op("dve", lambda e, xt=xt, si=si: e.tensor_scalar(xt, xt, stat[:, si:si + 1], None, ALU.mult),
                     reads=[B_stat[si]], writes=xtb)
                for jb in range(8):
                    bk = next_bank()
                    tr_group(bk, [(bank(bk)[:, i * 128:(i + 1) * 128], xt[:, (jb * 4 + i) * 128:(jb * 4 + i + 1) * 128],
                                   ident[:, :]) for i in range(4)], xtb)
                    eng = "act" if jb % 2 == 0 else "dve"

                    def ev(e, jb=jb, bk=bk, tt=tt, eng=eng):
                        inst = None
                        for i in range(4):
                            j = jb * 4 + i
                            o = hT3[:, j, tt * 128:(tt + 1) * 128]
                            s = bank(bk)[:, i * 128:(i + 1) * 128]
                            if eng == "act":
                                inst = e.activation(o, s, AF.Identity, bias=Bv_ap(g, k, j), scale=A_ap(g, k)[:, j:j + 1])
                            else:
                                inst = e.tensor_scalar(o, s, A_ap(g, k)[:, j:j + 1], Bv_ap(g, k, j), ALU.mult, ALU.add)
                        return inst
                    P.op(eng, ev, reads=[B_ABG, B_modT], writes=[PB[bk]] + hTb)
            return hT3, hTb

        def ffn_a(f, hT3, hTb):
            wd = wfi_d[f]
            nblk = 22
            ngrp = [0]

            def load(cb, half):
                ncol = 512 if cb < 21 else 256
                w, wb = RW.v(BF16, half * 32768, 32 * ncol)
                w3 = w.rearrange("p (k c) -> p k c", c=ncol)
                c0 = half * DFF + cb * 512
                dma("pool", w3, wd[:, c0:c0 + ncol].rearrange("(k p) c -> p k c", p=128), [], wb, f"RW{half}")
                return w3, wb

            cur = [load(0, 0), load(0, 1)]
            for cb in range(nblk):
                nct = 4 if cb < 21 else 2
                sg, sgb = RA.v(F32, 32768, 4 * NT)
                sg3 = sg.rearrange("p (c t) -> p c t", t=NT)
                stg, stgb = RA.v(BF16, 49152 + (cb % 2) * 8192, 4 * NT)
                stg3 = stg.rearrange("p (c t) -> p c t", t=NT)
                nxt = [None, None]
                for half in range(2):
                    w3, wb = cur[half]
                    for ct in range(nct):
                        for tb in range(2):
                            if ngrp[0] % 3 == 0:
                                mod_more(1)
                            ngrp[0] += 1
                            bk = next_bank()
                            mm_group(bank(bk), bk, [(w3[:, k, ct * 128:(ct + 1) * 128], hT3[:, k, tb * 512:(tb + 1) * 512])
                                                    for k in range(32)], list(wb) + hTb)
                            sgv = sg3[:, ct, tb * 512:(tb + 1) * 512]
                            sgvb = sgb[ct * 4 + tb * 2: ct * 4 + tb * 2 + 2]
                            if half == 0:
                                P.op("act", lambda e, sgv=sgv, bk=bk: e.activation(sgv, bank(bk), AF.Silu),
                                     writes=[PB[bk]] + sgvb)
                            else:
                                P.op("dve", lambda e, sgv=sgv, bk=bk, ct=ct, tb=tb, stg3=stg3:
                                     e.tensor_tensor(stg3[:, ct, tb * 512:(tb + 1) * 512], bank(bk), sgv, ALU.mult),
                                     reads=sgvb, writes=[PB[bk]] + stgb)
                    if cb + 1 < nblk:
                        nxt[half] = load(cb + 1, half)
                cur = nxt
                dma("sp", actT_d[cb * 4:cb * 4 + nct].rearrange("c p t -> p c t"), stg3[:, 0:nct, :], stgb,
                    B_actT[cb * 2:cb * 2 + 2], f"RAstg{cb % 2}")
            mod_more(1000)

        def post_core(acc_t, accb_t, Gb, Gbb, xt, xtb, chan, junk, junkb, dst_d, dst_b, row0):
            si = next_stat()
            P.op("act", lambda e: e.activation(junk, acc_t, AF.Square, accum_out=stat[:, si:si + 1]),
                 reads=accb_t, writes=junkb + [B_stat[si]])
            rstd_ops(si, D)
            P.op("dve", lambda e: e.scalar_tensor_tensor(acc_t, acc_t, stat[:, si:si + 1], Gb, ALU.mult, ALU.mult),
                 reads=[B_stat[si]] + Gbb, writes=accb_t)
            P.op("dve", lambda e: e.tensor_tensor(xt, xt, acc_t, ALU.add), reads=accb_t, writes=xtb)
            dma("sp", dst_d[row0:row0 + 128, :], xt, xtb, dst_b, chan)

        def post_tile(g, k, acc_t, accb_t, Gb, Gbb, xt, xtb, chan, src_d, src_b, dst_d, dst_b, row0):
            si = next_stat()
            P.op("act", lambda e: e.activation(xt, acc_t, AF.Square, accum_out=stat[:, si:si + 1]),
                 reads=accb_t, writes=xtb + [B_stat[si]])
            rstd_ops(si, D)
            dma("sp", xt, src_d[row0:row0 + 128, :], src_b, xtb, chan)
            P.op("dve", lambda e: e.scalar_tensor_tensor(acc_t, acc_t, stat[:, si:si + 1], Gb, ALU.mult, ALU.mult),
                 reads=[B_stat[si]] + Gbb, writes=accb_t)
            P.op("dve", lambda e: e.tensor_tensor(xt, xt, acc_t, ALU.add), reads=accb_t, writes=xtb)
            dma("sp", dst_d[row0:row0 + 128, :], xt, xtb, dst_b, chan)

        def ffn_b(f, g, k, src_d, src_bufs, dst_d, dst_bufs):
            wd = wfo_d[f]
            Gb, Gbb = build_Gb(g, k, RH, 32768, None, None)
            nsup = 11
            ctr = 0

            def xload(gt, sl):
                xt, xtb = RH.v(F32, sl * 16384, D)
                r0 = g * NT + gt * 128
                dma("sp", xt, src_d[r0:r0 + 128, :], [src_bufs[g][gt]] if src_bufs else [], xtb, f"RHx{sl}")

            for hf in range(2):
                acc, accb = RA.v(F32, 0, 4 * D)
                acc3 = acc.rearrange("p (t n) -> p t n", n=D)
                for nh in range(2):
                    for s in range(nsup):
                        ks = 8 if s < 10 else 6
                        slot = ctr % 2
                        ctr += 1
                        w, wb = RW.v(BF16, slot * 32768, ks * 2048)
                        w3 = w.rearrange("p (k n) -> p k n", n=2048)
                        dma("pool", w3, wd[s * 1024:s * 1024 + ks * 128, nh * 2048:(nh + 1) * 2048]
                            .rearrange("(k p) n -> p k n", p=128), [], wb, f"RW{slot}")
                        a, ab = RH.v(BF16, 49152 + slot * 8192, ks * 512)
                        a3 = a.rearrange("p (k t) -> p k t", t=512)
                        dma("sp", a3, actT_d[s * 8:s * 8 + ks, :, hf * 512:(hf + 1) * 512].rearrange("k p t -> p k t"),
                            B_actT[s * 4:s * 4 + 4], ab, f"RHa{slot}")
                        if nh == 1 and s == nsup - 1:
                            for tt in range(2):
                                xload(hf * 4 + tt, tt)
                        for tt in range(4):
                            bks = [next_bank() for _ in range(4)]

                            def fm(e, bks=bks, tt=tt, ks=ks, a3=a3, w3=w3):
                                inst = None
                                for kk in range(ks):
                                    for i in range(4):
                                        inst = e.matmul(bank(bks[i]), a3[:, kk, tt * 128:(tt + 1) * 128],
                                                        w3[:, kk, i * 512:(i + 1) * 512],
                                                        start=(kk == 0), stop=(kk == ks - 1))
                                return inst
                            P.op("pe", fm, reads=list(wb) + list(ab), writes=[PB[b] for b in bks])
                            for i in range(4):
                                nb = nh * 4 + i
                                o = acc3[:, tt, nb * 512:(nb + 1) * 512]
                                ob = accb[tt * 16 + nb * 2: tt * 16 + nb * 2 + 2]
                                if s == 0:
                                    P.op("act", lambda e, o=o, b=bks[i]: e.activation(o, bank(b), AF.Copy),
                                         writes=[PB[bks[i]]] + ob)
                                else:
                                    P.op("dve", lambda e, o=o, b=bks[i]: e.tensor_tensor(o, o, bank(b), ALU.add),
                                         writes=[PB[bks[i]]] + ob)
                for tt in range(4):
                    gt = hf * 4 + tt
                    xt, xtb = RH.v(F32, (tt % 2) * 16384, D)
                    junk, junkb = RH.v(BF16, 49152 + (tt % 2) * 8192, D)
                    post_core(acc3[:, tt, :], accb[tt * 16:(tt + 1) * 16], Gb, Gbb, xt, xtb, f"RHx{tt % 2}",
                              junk, junkb, dst_d, [dst_bufs[g][gt]] if dst_bufs else [], g * NT + gt * 128)
                    if tt + 2 < 4:
                        xload(gt + 2, tt % 2)

        dgt, B_dgt = ctile("dgt", [128, 512])

        def copy_op(eng, out_ap, in_ap, reads, writes):
            if eng == "act":
                P.op("act", lambda e: e.activation(out_ap, in_ap, AF.Copy), reads=reads, writes=writes)
            else:
                P.op("dve", lambda e: e.tensor_copy(out_ap, in_ap), reads=reads, writes=writes)

        def mixer1(g, hT3, hTb):
            nkeys = NT if g == 0 else NT + PAST
            koff = 0 if g == 0 else PAST
            qnT, qnTb = RW.v(BF16, 32768, 8 * NT)
            qnT3 = qnT.rearrange("p (k t) -> p k t", t=NT)
            ckvT, ckvTb = RW.v(BF16, 49152, 4 * 1536)
            ckvT3 = ckvT.rearrange("p (k t) -> p k t", t=1536)
            kpeT, kpeTb = RW.v(BF16, 61440, 1536, parts=64)
            gqa, gqab = RA.v(F32, 26624, QL)
            gkv, gkvb = RA.v(F32, 30720, KVL)
            dma("sp", gqa, gqa_d.partition_broadcast(128)[:, 0, :], [], gqab, "gqa")
            dma("sp", gkv, gkva_d.partition_broadcast(128)[:, 0, :], [], gkvb, "gkv")
            rk, rkb = RA.v(F32, 32768, 8 * 96)
            rk3 = rk.rearrange("p (t c) -> p t c", c=96)
            if g == 1:
                dma("sp", rk3, ropeK_d.rearrange("(t p) c -> p t c", p=128), [], rkb, "rk")
            rt, rtb = RA.v(F32, 36864, 3 * 64)
            junk, junkb = RA.v(BF16, 43008, QL)
            wring = [0]

            def wslot(ncol_bytes=16384):
                sl = wring[0] % 2
                wring[0] += 1
                return sl

            blocks = [(OFF_Q + 256 * i, 256, 256 * i) for i in range(6)] + [(OFF_KPE, 64, 1536)]
            for hf in range(2):
                Z, Zb = RA.v(F32, 0, 4 * 1600)
                Z3 = Z.rearrange("p (t c) -> p t c", c=1600)
                for (c0, ncol, zo) in blocks:
                    sl = wslot()
                    w, wb = RW.v(BF16, sl * 16384, 32 * ncol)
                    w3 = w.rearrange("p (k c) -> p k c", c=ncol)
                    dma("pool", w3, win_d[:, c0:c0 + ncol].rearrange("(k p) c -> p k c", p=128), [], wb, f"RWa{sl}")
                    for tt in range(4):
                        gt = hf * 4 + tt
                        bk = next_bank()
                        mm_group(bank(bk)[:, 0:ncol], bk, [(hT3[:, k, gt * 128:(gt + 1) * 128], w3[:, k, :])
                                                          for k in range(32)], list(wb) + hTb)
                        copy_op("act", Z3[:, tt, zo:zo + ncol], bank(bk)[:, 0:ncol], [], [PB[bk]] + Zb)
                for tt in range(4):
                    gt = hf * 4 + tt
                    for (o, n, gv, gvb) in ((0, QL, gqa, gqab), (QL, KVL, gkv, gkvb)):
                        si = next_stat()
                        zz = Z3[:, tt, o:o + n]
                        P.op("act", lambda e, zz=zz, si=si, n=n: e.activation(junk[:, 0:n], zz, AF.Square,
                                                                            accum_out=stat[:, si:si + 1]),
                             reads=Zb, writes=junkb + [B_stat[si]])
                        rstd_ops(si, n)
                        P.op("dve", lambda e, zz=zz, si=si, gv=gv: e.scalar_tensor_tensor(
                            zz, zz, stat[:, si:si + 1], gv, ALU.mult, ALU.mult),
                            reads=[B_stat[si]] + gvb, writes=Zb)
                    for jb in range(2):
                        bk = next_bank()
                        tr_group(bk, [(bank(bk)[:, i * 128:(i + 1) * 128],
                                       Z3[:, tt, (jb * 4 + i) * 128:(jb * 4 + i + 1) * 128], ident[:, :])
                                      for i in range(4)], Zb)
                        copy_op("dve" if jb else "act", qnT3[:, jb * 4:(jb + 1) * 4, gt * 128:(gt + 1) * 128],
                                bank(bk).rearrange("p (k t) -> p k t", t=128), [], [PB[bk]] + qnTb)
                    if g == 0:
                        dma("sp", sckv_d[gt * 128:(gt + 1) * 128, :], Z3[:, tt, QL:QL + KVL], Zb, [], "Zst")
                        dma("sp", skpe_d[gt * 128:(gt + 1) * 128, :], Z3[:, tt, 1536:1600], Zb, [], "Zst")
                    bk = next_bank()
                    tr_group(bk, [(bank(bk)[:, i * 128:(i + 1) * 128],
                                   Z3[:, tt, QL + i * 128:QL + (i + 1) * 128], ident[:, :]) for i in range(4)], Zb)
                    copy_op("act", ckvT3[:, :, koff + gt * 128:koff + (gt + 1) * 128],
                            bank(bk).rearrange("p (k t) -> p k t", t=128), [], [PB[bk]] + ckvTb)
                    ksrc = Z3[:, tt, 1536:1600]
                    ksrcb = Zb
                    if g == 1:
                        zk = Z3[:, tt, 1536:1600]
                        P.op("dve", lambda e, zk=zk, gt=gt: e.tensor_tensor(rt[:, 0:64], zk, rk3[:, gt, 0:64], ALU.mult),
                             reads=list(Zb) + rkb, writes=rtb)

                        def f2(e, zk=zk, gt=gt):
                            e.tensor_tensor(rt[:, 64:96], zk[:, 32:64], rk3[:, gt, 64:96], ALU.mult)
                            return e.tensor_tensor(rt[:, 96:128], zk[:, 0:32], rk3[:, gt, 64:96], ALU.mult)
                        P.op("dve", f2, reads=list(Zb) + rkb, writes=rtb)

                        def f3(e):
                            e.tensor_tensor(rt[:, 128:160], rt[:, 0:32], rt[:, 64:96], ALU.subtract)
                            return e.tensor_tensor(rt[:, 160:192], rt[:, 32:64], rt[:, 96:128], ALU.add)
                        P.op("dve", f3, reads=rtb, writes=rtb)
                        ksrc = rt[:, 128:192]
                        ksrcb = rtb
                    bk = next_bank()
                    tr_group(bk, [(bank(bk)[0:64, 0:128], ksrc, ident[:, :])], ksrcb)
                    copy_op("dve", kpeT[:, koff + gt * 128:koff + (gt + 1) * 128], bank(bk)[0:64, 0:128], [],
                            [PB[bk]] + kpeTb)
            if g == 1:
                for kt in range(4):
                    cc, ccb = RA.v(F32, 36864 + 1024 + (kt % 2) * 2560, 576)
                    dma("sp", cc[:, 0:512], cckv_d[kt * 128:(kt + 1) * 128, :], [], ccb, f"cc{kt % 2}")
                    dma("sp", cc[:, 512:576], ckpe_d[kt * 128:(kt + 1) * 128, :], [], ccb, f"cc{kt % 2}")
                    bk = next_bank()
                    tr_group(bk, [(bank(bk)[:, i * 128:(i + 1) * 128], cc[:, i * 128:(i + 1) * 128], ident[:, :])
                                  for i in range(4)], ccb)
                    copy_op("act", ckvT3[:, :, kt * 128:(kt + 1) * 128],
                            bank(bk).rearrange("p (k t) -> p k t", t=128), [], [PB[bk]] + ckvTb)
                    bk = next_bank()
                    tr_group(bk, [(bank(bk)[0:64, 0:128], cc[:, 512:576], ident[:, :])], ccb)
                    copy_op("dve", kpeT[:, kt * 128:(kt + 1) * 128], bank(bk)[0:64, 0:128], [], [PB[bk]] + kpeTb)

            nseq, L = (4, 256) if g == 0 else (1, 1024)
            Lp = L + 16
            npad = nseq * Lp
            pooledT, pooledTb = RA.v(BF16, 0, 16 * NT)
            pooledT3 = pooledT.rearrange("p (j t) -> p j t", t=NT)
            invb, invbb = RA.v(F32, 32768, NT)
            U = [RA.v(F32, 43008 + i * 4352, npad) for i in range(2)]
            Wt = [RA.v(F32, 51712 + i * 4352, npad) for i in range(3)]
            for (a, b) in U + Wt:
                P.op("dve", lambda e, a=a: e.memset(a, 0.0), writes=b)
            for cb in range(8):
                sl = wslot()
                w, wb = RW.v(BF16, sl * 16384, 32 * 256)
                w3 = w.rearrange("p (k c) -> p k c", c=256)
                dma("pool", w3, win_d[:, cb * 256:(cb + 1) * 256].rearrange("(k p) c -> p k c", p=128), [], wb,
                    f"RWa{sl}")
                for ct in range(2):
                    j = cb * 2 + ct
                    gi = j // 4
                    if j % 4 == 0:
                        dma("sp", invb, invc_d[g:g + 1, gi * NT:(gi + 1) * NT].partition_broadcast(128)[:, 0, :],
                            [], invbb, "invb")
                    u, ub = U[j % 2]
                    u3 = u.rearrange("p (s c) -> p s c", c=Lp)
                    for tb in range(2):
                        bk = next_bank()
                        mm_group(bank(bk), bk, [(w3[:, k, ct * 128:(ct + 1) * 128], hT3[:, k, tb * 512:(tb + 1) * 512])
                                                for k in range(32)], list(wb) + hTb)
                        if g == 0:
                            o = u3[:, tb * 2:(tb + 1) * 2, 8:8 + 256]
                            i_ = bank(bk).rearrange("p (s c) -> p s c", c=256)
                        else:
                            o = u3[:, 0, 8 + tb * 512:8 + (tb + 1) * 512]
                            i_ = bank(bk)
                        copy_op("act", o, i_, [], [PB[bk]] + ub)
                    cur, curb = u3, ub
                    sh = [(1, 0, 1), (2, 1, 1), (4, 2, 2), (8, 4, 4)]
                    for wi in range(gi + 1):
                        lo, ls, rs = sh[wi]
                        hi = Lp - (0 if wi == 0 else (1 if wi == 1 else (3 if wi == 2 else 7)))
                        nx, nxb = Wt[wi % 3]
                        nx3 = nx.rearrange("p (s c) -> p s c", c=Lp)
                        if wi == 0:
                            a0, a1 = cur[:, :, 0:Lp - 1], cur[:, :, 1:Lp]
                            oo = nx3[:, :, 1:Lp]
                        else:
                            a0, a1 = cur[:, :, lo - ls:hi - ls], cur[:, :, lo + rs:hi + rs]
                            oo = nx3[:, :, lo:hi]
                        P.op("dve", lambda e, oo=oo, a0=a0, a1=a1: e.tensor_tensor(oo, a0, a1, ALU.add),
                             reads=curb, writes=nxb)
                        cur, curb = nx3, nxb
                    tm, tmb = Wt[(gi + 1) % 3]
                    tm3 = tm.rearrange("p (s c) -> p s c", c=Lp)
                    P.op("dve", lambda e, tm3=tm3, cur=cur: e.tensor_tensor(
                        tm3[:, :, 8:8 + L], cur[:, :, 8:8 + L], invb.rearrange("p (s c) -> p s c", c=L), ALU.mult),
                        reads=list(curb) + invbb, writes=tmb)
                    P.op("dve", lambda e, tm3=tm3, u3=u3, j=j: e.tensor_tensor(
                        pooledT3[:, j, :].rearrange("p (s c) -> p s c", c=L), tm3[:, :, 8:8 + L], u3[:, :, 8:8 + L],
                        ALU.subtract), reads=list(tmb) + ub, writes=pooledTb)
            sl = wslot()
            wpg, wpgb = RW.v(BF16, sl * 16384, 16 * 512)
            wpg3 = wpg.rearrange("p (k c) -> p k c", c=512)
            dma("pool", wpg3, wpg_d.rearrange("(k p) c -> p k c", p=128), [], wpgb, f"RWa{sl}")
            for gi in range(4):
                for dt in range(4):
                    jo = gi * 4 + dt
                    ms, msb = RA.v(BF16, 36864 + (jo % 2) * 2048, NT)
                    for tb in range(2):
                        bk = next_bank()
                        mm_group(bank(bk), bk, [(wpg3[:, gi * 4 + c, dt * 128:(dt + 1) * 128],
                                                 pooledT3[:, gi * 4 + c, tb * 512:(tb + 1) * 512]) for c in range(4)],
                                 list(wpgb) + pooledTb)
                        P.op("act", lambda e, ms=ms, bk=bk, tb=tb, jo=jo: e.activation(
                            ms[:, tb * 512:(tb + 1) * 512], bank(bk), AF.Copy, scale=pscT[:, jo:jo + 1]),
                            reads=[B_vecT], writes=[PB[bk]] + msb)
                    dma("sp", mix_d[jo], ms, msb, [B_mix[jo]], f"ms{jo % 2}")

            for cb in range(32):
                sl = wslot()
                w, wb = RW.v(BF16, sl * 16384, 32 * 256)
                w3 = w.rearrange("p (k c) -> p k c", c=256)
                c0 = OFF_GP + cb * 256
                dma("pool", w3, win_d[:, c0:c0 + 256].rearrange("(k p) c -> p k c", p=128), [], wb, f"RWa{sl}")
                for ct in range(2):
                    jg = cb * 2 + ct
                    gs, gsb = RA.v(F32, 43008 + (jg % 2) * 4096, NT)
                    for tb in range(2):
                        bk = next_bank()
                        mm_group(bank(bk), bk, [(w3[:, k, ct * 128:(ct + 1) * 128], hT3[:, k, tb * 512:(tb + 1) * 512])
                                                for k in range(32)], list(wb) + hTb)
                        P.op("act", lambda e, gs=gs, bk=bk, tb=tb: e.activation(
                            gs[:, tb * 512:(tb + 1) * 512], bank(bk), AF.Sigmoid), writes=[PB[bk]] + gsb)
                    dma("sp", sg_d[jg], gs, gsb, [B_sg[jg]], f"gs{jg % 2}")
            return qnT3, qnTb, ckvT3, ckvTb, kpeT, kpeTb

        def attention(g, qnT3, qnTb, ckvT3, ckvTb, kpeT, kpeTb):
            nkt = 8 if g == 0 else 12
            wkv, wkvb_ = RW.v(BF16, 0, 4 * 4096)
            wkv3 = wkv.rearrange("p (k c) -> p k c", c=4096)
            wkv4 = wkv.rearrange("p (k h c) -> p k h c", h=NH, c=256)
            dma("pool", wkv3, wkvb_d.rearrange("(k p) c -> p k c", p=128), [], wkvb_, "RWkv")
            vall, vallb = RH.v(BF16, 0, nkt * 2048)
            vall3 = vall.rearrange("p (k c) -> p k c", c=2048)
            oT, oTb = RA.v(BF16, 0, 16 * NT)
            oT3 = oT.rearrange("p (h t) -> p h t", t=NT)
            for kt in range(nkt):
                for hg in range(4):
                    bk = next_bank()
                    mm_group(bank(bk).rearrange("p (h c) -> p h c", c=128), bk,
                             [(ckvT3[:, k, kt * 128:(kt + 1) * 128], wkv4[:, k, hg * 4:(hg + 1) * 4, 128:256])
                              for k in range(4)], list(wkvb_) + ckvTb)
                    copy_op("act" if hg % 2 else "dve", vall3[:, kt, hg * 512:(hg + 1) * 512], bank(bk), [],
                            [PB[bk]] + vallb)
            if g == 1:
                cosT, cosTb = RA.v(F32, 47104, NT, parts=64)
                sinT, sinTb = RA.v(F32, 51200, NT, parts=64)
                dma("sp", cosT, ropeT_d[0], [], cosTb, "cosT")
                dma("sp", sinT, ropeT_d[1], [], sinTb, "sinT")
            qblocks = [(s * 256, 256, [2 * s, 2 * s + 1]) for s in range(4)] if g == 0 else \
                      [(0, 512, list(range(12))), (512, 512, list(range(12)))]
            qbi = 0
            for h in range(NH):
                hs = h % 2
                wq, wqb_ = RH.v(BF16, 49152 + hs * 3072, 8 * 192)
                wq3 = wq.rearrange("p (k c) -> p k c", c=192)
                dma("pool", wq3, wqb_d[:, h * 192:(h + 1) * 192].rearrange("(k p) c -> p k c", p=128), [], wqb_,
                    f"wq{hs}")
                QN, QNb = RA.v(BF16, 32768 + hs * 2048, NT)
                QP, QPb = RA.v(BF16, 36864 + hs * 2048, NT, parts=64)
                KN, KNb = RA.v(BF16, 40960 + hs * 3072, 1536)
                if g == 1:
                    wr, wrb = RH.v(BF16, 55296 + hs * 1024, 8 * 64)
                    wr3 = wr.rearrange("p (k c) -> p k c", c=64)
                    c0 = h * 192 + 128
                    dma("pool", wr3[:, :, 0:32], wqb_d[:, c0 + 32:c0 + 64].rearrange("(k p) c -> p k c", p=128), [],
                        wrb, f"wr{hs}")
                    dma("pool", wr3[:, :, 32:64], wqb_d[:, c0:c0 + 32].rearrange("(k p) c -> p k c", p=128), [],
                        wrb, f"wr{hs}")
                for tb in range(2):
                    bk = next_bank()
                    mm_group(bank(bk), bk, [(wq3[:, k, 0:128], qnT3[:, k, tb * 512:(tb + 1) * 512]) for k in range(8)],
                             list(wqb_) + qnTb)
                    copy_op("act", QN[:, tb * 512:(tb + 1) * 512], bank(bk), [], [PB[bk]] + QNb)
                    bk = next_bank()
                    mm_group(bank(bk)[0:64, :], bk, [(wq3[:, k, 128:192], qnT3[:, k, tb * 512:(tb + 1) * 512])
                                                     for k in range(8)], list(wqb_) + qnTb)
                    if g == 0:
                        copy_op("dve", QP[:, tb * 512:(tb + 1) * 512], bank(bk)[0:64, :], [], [PB[bk]] + QPb)
                    else:
                        bk2 = next_bank()
                        mm_group(bank(bk2)[0:64, :], bk2, [(wr3[:, k, :], qnT3[:, k, tb * 512:(tb + 1) * 512])
                                                           for k in range(8)], list(wrb) + qnTb)
                        t1, t1b = RA.v(F32, 55296 + tb * 4096, 512, parts=64)
                        t2, t2b = RA.v(F32, 57344 + tb * 4096, 512, parts=64)
                        P.op("dve", lambda e, t1=t1, bk=bk, tb=tb: e.tensor_tensor(
                            t1, bank(bk)[0:64, :], cosT[:, tb * 512:(tb + 1) * 512], ALU.mult),
                            reads=cosTb, writes=[PB[bk]] + t1b)
                        P.op("dve", lambda e, t2=t2, bk2=bk2, tb=tb: e.tensor_tensor(
                            t2, bank(bk2)[0:64, :], sinT[:, tb * 512:(tb + 1) * 512], ALU.mult),
                            reads=sinTb, writes=[PB[bk2]] + t2b)
                        P.op("dve", lambda e, t1=t1, t2=t2, QP=QP, tb=tb: e.tensor_tensor(
                            QP[:, tb * 512:(tb + 1) * 512], t1, t2, ALU.add), reads=list(t1b) + list(t2b), writes=QPb)
                for kb in range(nkt // 4):
                    bk = next_bank()
                    mm_group(bank(bk), bk, [(wkv3[:, k, h * 256:h * 256 + 128], ckvT3[:, k, kb * 512:(kb + 1) * 512])
                                            for k in range(4)], list(wkvb_) + ckvTb)
                    copy_op("dve" if kb % 2 else "act", KN[:, kb * 512:(kb + 1) * 512], bank(bk), [], [PB[bk]] + KNb)
                for (q0, nq, kts) in qblocks:
                    ob, sb_ = 4 + qbi % 2, 6 + qbi % 2
                    qbi += 1
                    pend = []
                    nk = len(kts)

                    def emit_pv(item, nq=nq, ob=ob, sb_=sb_, h=h, nk=nk):
                        pi, pkt, pPT, pPTb = item

                        def fpv(e):
                            e.matmul(bank(ob)[:, 0:nq], vall3[:, pkt, h * 128:(h + 1) * 128], pPT[:, 0:nq],
                                     start=(pi == 0), stop=(pi == nk - 1))
                            return e.matmul(bank(sb_)[:, 0:nq], onesb[:, :], pPT[:, 0:nq],
                                            start=(pi == 0), stop=(pi == nk - 1))
                        P.op("pe", fpv, reads=list(pPTb) + vallb + [B_onesb], writes=[PB[ob], PB[sb_]])

                    for i, kt in enumerate(kts):
                        sbk = i % 4
                        PT, PTb = RH.v(BF16, 57344 + (i % 4) * 1024, 512)

                        def fs(e, sbk=sbk, kt=kt, q0=q0, nq=nq, KN=KN, QN=QN, QP=QP):
                            e.matmul(bank(sbk)[:, 0:nq], KN[:, kt * 128:(kt + 1) * 128], QN[:, q0:q0 + nq],
                                     start=True, stop=False)
                            return e.matmul(bank(sbk)[:, 0:nq], kpeT[:, kt * 128:(kt + 1) * 128], QP[:, q0:q0 + nq],
                                            start=False, stop=True)
                        P.op("pe", fs, reads=list(KNb) + list(QNb) + list(QPb) + kpeTb, writes=[PB[sbk]])
                        P.op("act", lambda e, PT=PT, sbk=sbk, nq=nq: e.activation(
                            PT[:, 0:nq], bank(sbk)[:, 0:nq], AF.Exp, scale=float(SCALE)),
                            writes=[PB[sbk]] + PTb)
                        pend.append((i, kt, PT, PTb))
                        if len(pend) > 2:
                            emit_pv(pend.pop(0))
                    while pend:
                        emit_pv(pend.pop(0))
                    ri, rib = RH.v(F32, 61440, 512)
                    P.op("dve", lambda e, ri=ri, sb_=sb_, nq=nq: e.reciprocal(ri[:, 0:nq], bank(sb_)[:, 0:nq]),
                         writes=[PB[sb_]] + rib)
                    P.op("dve", lambda e, ri=ri, ob=ob, nq=nq, h=h, q0=q0: e.tensor_tensor(
                        oT3[:, h, q0:q0 + nq], bank(ob)[:, 0:nq], ri[:, 0:nq], ALU.mult),
                        reads=rib, writes=[PB[ob]] + oTb)
            return oT3, oTb

        def comb_wo(g, oT3, oTb):
            mixT, mixTb = RA.v(BF16, 32768, 16 * NT)
            mixT3 = mixT.rearrange("p (k t) -> p k t", t=NT)
            dma("sp", mixT3, mix_d.rearrange("k p t -> p k t"), B_mix, mixTb, "mixT")
            cT, cTb = RH.v(BF16, 0, 32 * NT)
            cT3 = cT.rearrange("p (k t) -> p k t", t=NT)
            for cb in range(16):
                sl = cb % 2
                wp, wpb = RW.v(BF16, sl * 16384, 16 * 256)
                wa, wab = RW.v(BF16, sl * 16384 + 8192, 16 * 256)
                wp3 = wp.rearrange("p (k c) -> p k c", c=256)
                wa3 = wa.rearrange("p (k c) -> p k c", c=256)
                dma("pool", wp3, wpo_d[:, cb * 256:(cb + 1) * 256].rearrange("(k p) c -> p k c", p=128), [], wpb,
                    f"RWa{sl}")
                dma("pool", wa3, wao_d[:, cb * 256:(cb + 1) * 256].rearrange("(k p) c -> p k c", p=128), [], wab,
                    f"RWb{sl}")
                for ct in range(2):
                    j = cb * 2 + ct
                    gp, gpb = RW.v(F32, 32768 + (j % 2) * 8192, NT)
                    ga, gab = RW.v(F32, 32768 + (j % 2) * 8192 + 4096, NT)
                    dma("sp", gp, sg_d[j], [B_sg[j]], gpb, f"gp{j % 2}")
                    dma("sp", ga, sg_d[32 + j], [B_sg[32 + j]], gab, f"ga{j % 2}")
                    for tb in range(2):
                        b1, b2 = next_bank(), next_bank()
                        mm_group(bank(b1), b1, [(wp3[:, k, ct * 128:(ct + 1) * 128], mixT3[:, k, tb * 512:(tb + 1) * 512])
                                                for k in range(16)], list(wpb) + mixTb)
                        mm_group(bank(b2), b2, [(wa3[:, k, ct * 128:(ct + 1) * 128], oT3[:, k, tb * 512:(tb + 1) * 512])
                                                for k in range(16)], list(wab) + oTb)
                        t1, t1b = RW.v(F32, 49152 + tb * 4096, 512)
                        t2, t2b = RW.v(F32, 51200 + tb * 4096, 512)
                        P.op("dve", lambda e, t1=t1, b1=b1, gp=gp, tb=tb: e.tensor_tensor(
                            t1, bank(b1), gp[:, tb * 512:(tb + 1) * 512], ALU.mult), reads=gpb, writes=[PB[b1]] + t1b)
                        P.op("dve", lambda e, t2=t2, b2=b2, ga=ga, tb=tb: e.tensor_tensor(
                            t2, bank(b2), ga[:, tb * 512:(tb + 1) * 512], ALU.mult), reads=gab, writes=[PB[b2]] + t2b)
                        P.op("dve", lambda e, t1=t1, t2=t2, j=j, tb=tb: e.tensor_tensor(
                            cT3[:, j, tb * 512:(tb + 1) * 512], t1, t2, ALU.add),
                            reads=list(t1b) + list(t2b), writes=cTb)
            Gb, Gbb = build_Gb(g, 1, RA, 49152, None, None)
            for q in range(4):
                Y, Yb = RA.v(F32, 0, 2 * D)
                Y3 = Y.rearrange("p (t n) -> p t n", n=D)
                for nb in range(8):
                    sl = (q * 8 + nb) % 2
                    w, wb = RW.v(BF16, sl * 32768, 32 * 512)
                    w3 = w.rearrange("p (k c) -> p k c", c=512)
                    dma("pool", w3, wo_d[:, nb * 512:(nb + 1) * 512].rearrange("(k p) c -> p k c", p=128), [], wb,
                        f"RW{sl}")
                    for tt in range(2):
                        gt = q * 2 + tt
                        bk = next_bank()
                        mm_group(bank(bk), bk, [(cT3[:, k, gt * 128:(gt + 1) * 128], w3[:, k, :]) for k in range(32)],
                                 list(wb) + cTb)
                        copy_op("act", Y3[:, tt, nb * 512:(nb + 1) * 512], bank(bk), [],
                                [PB[bk]] + Yb[tt * 16 + nb * 2:tt * 16 + nb * 2 + 2])
                for tt in range(2):
                    gt = q * 2 + tt
                    xt, xtb = RA.v(F32, 32768, D)
                    post_tile(g, 1, Y3[:, tt, :], Yb[tt * 16:(tt + 1) * 16], Gb, Gbb, xt, xtb, "RAxw",
                              x1_d, [B_x1[g][gt]], x2_d, [B_x2[g][gt]], g * NT + gt * 128)

        stage0()
        for g in range(2):
            hT3, hTb = pre(g, 0, x_d, None)
            ffn_a(0, hT3, hTb)
            ffn_b(0, g, 0, x_d, None, x1_d, B_x1)
            if stop_after == "ffn1":
                continue
            hT3, hTb = pre(g, 1, x1_d, B_x1)
            mx = mixer1(g, hT3, hTb)
            if stop_after == "mixer1":
                continue
            oT3, oTb = attention(g, *mx)
            comb_wo(g, oT3, oTb)
            if stop_after == "mixer":
                continue
            hT3, hTb = pre(g, 2, x2_d, B_x2)
            ffn_a(1, hT3, hTb)
            ffn_b(1, g, 2, x2_d, B_x2, y_d, None)
        P.emit(nc, st)
    return nc


def _consts():
    ident = np.eye(128, dtype=np.float32)
    t = np.arange(NT)
    row = (t // 64).astype(np.float32)
    col = (t % 64).astype(np.float32)
    inv = (10000.0 ** (-np.arange(16, dtype=np.float32) / 16)).astype(np.float32)
    ang = np.concatenate([row[:, None] * inv, col[:, None] * inv], axis=-1).astype(np.float32)
    cos, sin = np.cos(ang).astype(np.float32), np.sin(ang).astype(np.float32)
    ropeT = np.stack([np.concatenate([cos.T, cos.T], 0), np.concatenate([-sin.T, sin.T], 0)]).astype(np.float32)
    ropeK = np.concatenate([cos, cos, sin], axis=1).astype(np.float32)
    invc = np.zeros((2, 4, NT), np.float32)
    for g, L in enumerate((256, 1024)):
        tt = np.arange(NT) % L
        for wi, w in enumerate(WINS):
            lo = np.clip(tt - w // 2, 0, L)
            hi = np.clip(tt + w // 2, 0, L)
            invc[g, wi] = 1.0 / (hi - lo)
    return ident, ropeT, ropeK, invc.reshape(2, 4 * NT)


def make_in_maps(inputs):
    f = lambda a: np.ascontiguousarray(np.asarray(a, dtype=np.float32))
    ident, ropeT, ropeK, invc = _consts()
    shared = {
        "w_mod": f(inputs["w_mod"][0]),
        "b_mod": f(inputs["b_mod"][0]).reshape(288, 128),
        "g_pre": f(inputs["g_pre"][0]).reshape(96, 128),
        "g_post": f(inputs["g_post"][0]).reshape(96, 128),
        "w_f1i": f(inputs["w_ffn1_in"][0]), "w_f1o": f(inputs["w_ffn1_out"][0]),
        "w_f2i": f(inputs["w_ffn2_in"][0]), "w_f2o": f(inputs["w_ffn2_out"][0]),
        "w_in": f(inputs["w_in"][0]),
        "g_qa": f(inputs["g_qa"][0]).reshape(1, QL),
        "w_qb": f(inputs["w_qb"][0]),
        "g_kva": f(inputs["g_kva"][0]).reshape(1, KVL),
        "w_kvb": f(inputs["w_kvb"][0]),
        "w_pg": f(inputs["w_pool_grp"][0]).reshape(2048, 512),
        "pool_scale": f(inputs["pool_scale"][0]).reshape(16, 128),
        "w_po": f(inputs["w_pool_out"][0]), "w_ao": f(inputs["w_attn_out"][0]), "w_o": f(inputs["w_o"][0]),
        "ident": ident, "ropeT": ropeT, "ropeK": ropeK, "invc": invc,
    }
    xp = f(inputs["x_prompt"])
    xs = f(inputs["x_sample"])
    maps = []
    for c in range(8):
        m = dict(shared)
        m["x"] = np.concatenate([xp[4 * c:4 * c + 4].reshape(NT, D), xs[c]], axis=0)
        m["cckv"] = f(inputs["cache_ckv"][c, 0])
        m["ckpe"] = f(inputs["cache_kpe"][c, 0])
        m["cvec"] = np.stack([f(inputs["c_ctx"]), f(inputs["c"][c])], axis=0)
        maps.append(m)
    return maps


_NC_CACHE = {}


def kernel(**inputs):
    if "nc" not in _NC_CACHE:
        _NC_CACHE["nc"] = build_nc()
    nc = _NC_CACHE["nc"]
    maps = make_in_maps(inputs)
    res = run_bass_kernel_spmd(nc, maps, core_ids=list(range(8)))
    r = res.results
    y = [np.asarray(r[c]["y"]) for c in range(8)]
    y_prompt = np.concatenate([y[c][:NT].reshape(4, 256, D) for c in range(8)], axis=0)
    y_sample = np.stack([y[c][NT:] for c in range(8)], axis=0)
    s_ckv = np.concatenate([np.asarray(r[c]["sckv"]).reshape(4, 1, 256, KVL) for c in range(8)], axis=0)
    s_kpe = np.concatenate([np.asarray(r[c]["skpe"]).reshape(4, 1, 256, ROPE) for c in range(8)], axis=0)
    return (y_prompt.astype(np.float32), y_sample.astype(np.float32),
            s_ckv.astype(np.float32), s_kpe.astype(np.float32))
```
